# Optimizing a Trainium2 kernel written in Bass

```python
import math
import jax, jax.numpy as jnp
from jax import lax
import numpy as np

D_MODEL = 4096
BATCH = 2
SEQ = 8192
DEPTH = 4

CHUNK = 64
SB_BLOCK = 128
N_HEADS_SB = 16
HEAD_DIM_SB = D_MODEL // 2 // N_HEADS_SB
N_GROUPS_SGU = 16
GROUP_DIM_SGU = D_MODEL // 2 // N_GROUPS_SGU
SGU_LEN = 128
N_HEADS_GDN = 32
HEAD_DIM_GDN = D_MODEL // N_HEADS_GDN
GDN_CONV = 4
D_FF = 2 * D_MODEL
FFN_CONV = 3
N_MOD = 6
N_EVEN = (DEPTH + 1) // 2
N_ODD = DEPTH // 2
EPS = 1e-6

kernel_name = "hybrid_stickbreak_sgu_gdn_streaming_encoder"


def rms_norm(x, gain):
    xf = x.astype(jnp.float32)
    y = xf * lax.rsqrt(jnp.mean(xf * xf, axis=-1, keepdims=True) + EPS)
    return (y * gain.astype(jnp.float32)).astype(x.dtype)


def l2_normalize(t):
    tf = t.astype(jnp.float32)
    return tf * lax.rsqrt(jnp.sum(tf * tf, axis=-1, keepdims=True) + EPS)


def causal_dwconv(x, w):
    K = w.shape[0]
    S_ = x.shape[1]
    xp = jnp.pad(x, ((0, 0), (K - 1, 0), (0, 0)))
    out = xp[:, 0:S_] * w[0]
    for j in range(1, K):
        out = out + xp[:, j:j + S_] * w[j]
    return out


def stick_breaking_attention(q, k, v):
    B_, S_, H_, dh = q.shape
    nb = S_ // SB_BLOCK
    qb = jnp.moveaxis(q.reshape(B_, nb, SB_BLOCK, H_, dh), 1, 0)
    key_pos = jnp.arange(S_)
    scale = dh ** -0.5

    def one_block(args):
        q_blk, blk = args
        z = jnp.einsum('bqhd,bkhd->bhqk', q_blk, k, preferred_element_type=jnp.float32) * scale
        q_pos = blk * SB_BLOCK + jnp.arange(SB_BLOCK)
        strict = key_pos[None, :] < q_pos[:, None]
        log_1m = jnp.where(strict, jax.nn.log_sigmoid(-z), 0.0)
        tail = lax.cumsum(log_1m, axis=3, reverse=True) - log_1m
        wts = jnp.where(strict, jnp.exp(jax.nn.log_sigmoid(z) + tail), 0.0)
        return jnp.einsum('bhqk,bkhd->bqhd', wts.astype(v.dtype), v)

    out = lax.map(one_block, (qb, jnp.arange(nb)))
    return jnp.moveaxis(out, 0, 1).reshape(B_, S_, H_, dh)


def spatial_gating_unit(u, v, gain, w_s, b_s):
    B_, S_, G, dg = u.shape
    u = jax.nn.gelu(u)
    v = rms_norm(jax.nn.gelu(v), gain)
    vc = v.reshape(B_, S_ // SGU_LEN, SGU_LEN, G, dg)
    causal = jnp.tril(jnp.ones((SGU_LEN, SGU_LEN), dtype=bool))
    w = jnp.where(causal, w_s, 0).astype(v.dtype)
    mixed = jnp.einsum('gts,bnsgc->bntgc', w, vc) + b_s.T[None, None, :, :, None].astype(v.dtype)
    return u * mixed.reshape(B_, S_, G, dg)


def even_mixer(h, w_in, w_out, sgu_gain, sgu_w, sgu_b):
    B_, S_, _ = h.shape
    proj = h @ w_in
    q, k, v, u, vg = jnp.split(proj, 5, axis=-1)
    sb = lambda t: t.reshape(B_, S_, N_HEADS_SB, HEAD_DIM_SB)
    sg = lambda t: t.reshape(B_, S_, N_GROUPS_SGU, GROUP_DIM_SGU)
    o_a = stick_breaking_attention(sb(q), sb(k), sb(v)).reshape(B_, S_, D_MODEL // 2)
    o_b = spatial_gating_unit(sg(u), sg(vg), sgu_gain, sgu_w, sgu_b).reshape(B_, S_, D_MODEL // 2)
    return jnp.concatenate([o_a, o_b], axis=-1) @ w_out


def chunk_gated_delta_rule(q, k, v, g, beta):
    B_, S_, H_, dk = q.shape
    dv = v.shape[-1]
    N = S_ // CHUNK

    def chunks(t):
        t = t.astype(jnp.float32).reshape((B_, N, CHUNK, H_) + t.shape[3:])
        return jnp.moveaxis(t, 3, 1)

    q, k, v, g, beta = chunks(q), chunks(k), chunks(v), chunks(g), chunks(beta)
    gc = jnp.cumsum(g, axis=-1)
    idx = jnp.arange(CHUNK)
    lower_incl = idx[:, None] >= idx[None, :]
    strict = idx[:, None] > idx[None, :]
    decay = jnp.exp(jnp.where(lower_incl, gc[..., :, None] - gc[..., None, :], -jnp.inf))
    kb = k * beta[..., None]
    a = jnp.where(strict, jnp.einsum('bhnid,bhnjd->bhnij', kb, k) * decay, 0.0)
    eye = jnp.eye(CHUNK, dtype=jnp.float32)
    t_inv = lax.linalg.triangular_solve(a + eye, jnp.broadcast_to(eye, a.shape),
                                        left_side=True, lower=True, unit_diagonal=True)
    u = jnp.einsum('bhnij,bhnjd->bhnid', t_inv, v * beta[..., None])
    w = jnp.einsum('bhnij,bhnjd->bhnid', t_inv, kb * jnp.exp(gc)[..., None])
    qk = jnp.einsum('bhnid,bhnjd->bhnij', q, k) * decay
    q_dec = q * jnp.exp(gc)[..., None]
    k_dec = k * jnp.exp(gc[..., -1:] - gc)[..., None]
    g_tot = jnp.exp(gc[..., -1])
    xs = (jnp.moveaxis(w, 2, 0), jnp.moveaxis(u, 2, 0), jnp.moveaxis(q_dec, 2, 0),
          jnp.moveaxis(qk, 2, 0), jnp.moveaxis(k_dec, 2, 0), jnp.moveaxis(g_tot, 2, 0))

    def step(state, inp):
        w_c, u_c, qd_c, qk_c, kd_c, gt_c = inp
        v_new = u_c - jnp.einsum('bhcd,bhde->bhce', w_c, state)
        o_c = jnp.einsum('bhcd,bhde->bhce', qd_c, state) + jnp.einsum('bhij,bhje->bhie', qk_c, v_new)
        state = state * gt_c[..., None, None] + jnp.einsum('bhcd,bhce->bhde', kd_c, v_new)
        return state, o_c

    state0 = jnp.zeros((B_, H_, dk, dv), jnp.float32)
    _, o = lax.scan(step, state0, xs)
    return jnp.transpose(o, (1, 0, 3, 2, 4)).reshape(B_, S_, H_, dv)


def gated_deltanet_mixer(h, w_in, conv_w, a_log, dt_bias, o_gain, w_out):
    B_, S_, _ = h.shape
    H_, dh = N_HEADS_GDN, HEAD_DIM_GDN
    proj = h @ w_in
    qkv = proj[..., :3 * D_MODEL]
    z = proj[..., 3 * D_MODEL:4 * D_MODEL]
    a = proj[..., 4 * D_MODEL:4 * D_MODEL + H_]
    b = proj[..., 4 * D_MODEL + H_:]
    qkv = jax.nn.silu(causal_dwconv(qkv, conv_w)).reshape(B_, S_, 3, H_, dh)
    q = l2_normalize(qkv[:, :, 0]) * (dh ** -0.5)
    k = l2_normalize(qkv[:, :, 1])
    v = qkv[:, :, 2]
    beta = jax.nn.sigmoid(b.astype(jnp.float32))
    g = -jnp.exp(a_log.astype(jnp.float32)) * jax.nn.softplus(a.astype(jnp.float32) + dt_bias.astype(jnp.float32))
    o = chunk_gated_delta_rule(q, k, v, g, beta)
    o = rms_norm(o, o_gain) * jax.nn.silu(z.reshape(B_, S_, H_, dh).astype(jnp.float32))
    return o.reshape(B_, S_, D_MODEL).astype(h.dtype) @ w_out


def conv_glu_ffn(h, w_up, conv_w, conv_b, w_down):
    gate, val = jnp.split(h @ w_up, 2, axis=-1)
    gate = causal_dwconv(gate, conv_w) + conv_b
    return (jax.nn.gelu(gate) * val) @ w_down


def setup_inputs(seed: int = 0) -> dict:
    key = jax.random.key(seed)
    ks = jax.random.split(key, 24)
    f32 = jnp.float32
    D = D_MODEL

    def normal(k, shape, scale):
        return jax.random.normal(k, shape, f32) * scale

    def gain(k, shape):
        return 1.0 + 0.05 * jax.random.normal(k, shape, f32)

    dt = jnp.exp(jax.random.uniform(ks[16], (N_ODD, N_HEADS_GDN), f32, math.log(1e-3), math.log(1e-1)))
    return {
        "x": normal(ks[0], (BATCH, SEQ, D), 1.0),
        "c": normal(ks[1], (BATCH, D), 1.0),
        "ada_w": normal(ks[2], (D, N_MOD * D), 0.3 * D ** -0.5),
        "ada_b": normal(ks[3], (N_MOD * D,), 0.02),
        "ada_layer": normal(ks[4], (DEPTH, N_MOD, D), 0.1),
        "norm_mix": gain(ks[5], (DEPTH, D)),
        "norm_ffn": gain(ks[6], (DEPTH, D)),
        "norm_final": gain(ks[7], (D,)),
        "ev_w_in": normal(ks[8], (N_EVEN, D, 5 * (D // 2)), D ** -0.5),
        "ev_w_out": normal(ks[9], (N_EVEN, D, D), D ** -0.5),
        "sgu_gain": gain(ks[10], (N_EVEN, N_GROUPS_SGU, GROUP_DIM_SGU)),
        "sgu_w": normal(ks[11], (N_EVEN, N_GROUPS_SGU, SGU_LEN, SGU_LEN), SGU_LEN ** -0.5),
        "sgu_b": gain(ks[12], (N_EVEN, N_GROUPS_SGU, SGU_LEN)),
        "gdn_w_in": normal(ks[13], (N_ODD, D, 4 * D + 2 * N_HEADS_GDN), D ** -0.5),
        "gdn_conv": normal(ks[14], (N_ODD, GDN_CONV, 3 * D), GDN_CONV ** -0.5),
        "gdn_a_log": jnp.log(jax.random.uniform(ks[15], (N_ODD, N_HEADS_GDN), f32, 1.0, 16.0)),
        "gdn_dt_bias": dt + jnp.log(-jnp.expm1(-dt)),
        "gdn_o_gain": gain(ks[17], (N_ODD, HEAD_DIM_GDN)),
        "gdn_w_out": normal(ks[18], (N_ODD, D, D), D ** -0.5),
        "ffn_w_up": normal(ks[19], (DEPTH, D, 2 * D_FF), D ** -0.5),
        "ffn_conv": normal(ks[20], (DEPTH, FFN_CONV, D_FF), FFN_CONV ** -0.5),
        "ffn_conv_b": normal(ks[21], (DEPTH, D_FF), 0.02),
        "ffn_w_down": normal(ks[22], (DEPTH, D_FF, D), D_FF ** -0.5),
    }


def reference(x, c, ada_w, ada_b, ada_layer, norm_mix, norm_ffn, norm_final,
              ev_w_in, ev_w_out, sgu_gain, sgu_w, sgu_b,
              gdn_w_in, gdn_conv, gdn_a_log, gdn_dt_bias, gdn_o_gain, gdn_w_out,
              ffn_w_up, ffn_conv, ffn_conv_b, ffn_w_down):
    B_ = x.shape[0]
    mod_all = (jax.nn.silu(c) @ ada_w + ada_b).reshape(B_, N_MOD, D_MODEL)
    for layer in range(DEPTH):
        mod = mod_all + ada_layer[layer]
        sh_m, sc_m, gt_m, sh_f, sc_f, gt_f = (mod[:, i, None, :] for i in range(N_MOD))
        h = rms_norm(x, norm_mix[layer]) * (1 + sc_m) + sh_m
        if layer % 2 == 0:
            e = layer // 2
            y = even_mixer(h, ev_w_in[e], ev_w_out[e], sgu_gain[e], sgu_w[e], sgu_b[e])
        else:
            o = layer // 2
            y = gated_deltanet_mixer(h, gdn_w_in[o], gdn_conv[o], gdn_a_log[o], gdn_dt_bias[o],
                                     gdn_o_gain[o], gdn_w_out[o])
        x = x + gt_m * y
        h = rms_norm(x, norm_ffn[layer]) * (1 + sc_f) + sh_f
        x = x + gt_f * conv_glu_ffn(h, ffn_w_up[layer], ffn_conv[layer], ffn_conv_b[layer], ffn_w_down[layer])
    return rms_norm(x, norm_final)
```

```python
import numpy as np
import ml_dtypes
from contextlib import ExitStack
import concourse.bass as bass
import concourse.mybir as mybir
from concourse.bass_utils import run_bass_kernel_spmd

F32 = mybir.dt.float32
BF16 = mybir.dt.bfloat16
AF = mybir.ActivationFunctionType
ALU = mybir.AluOpType
EPS = 1e-6
NEG = -1.0e30


class Cfg:
    def __init__(s, NC=8, D=4096, B=2, S=8192, DEPTH=4):
        s.NC, s.D, s.B, s.S, s.DEPTH = NC, D, B, S, DEPTH
        s.T = B * S
        s.KC = D // 128
        s.FPC = D // NC
        s.FC = s.FPC // 128
        s.HS = D // 2 // 128 // NC
        s.HG = D // 128 // NC
        s.DFF = 2 * D
        s.FFPC = s.DFF // NC
        s.FFC = s.FFPC // 128
        s.KC2 = s.DFF // 128
        s.TT = 512
        s.NT = s.T // s.TT
        s.NE = (DEPTH + 1) // 2
        s.NO = DEPTH // 2
        assert s.HS >= 1 and S % 512 == 0


class Sem:
    def __init__(s, h, key, dma=False):
        s.h, s.key, s.cnt, s.dma = h, key, 0, dma


class R:
    def __init__(s, name=""):
        s.name = name
        s.w = None
        s.r = {}
        s.dsem = None


class Tl:
    def __init__(s, t, name):
        s.t = t
        s.R = R(name)

    def __getitem__(s, k):
        return s.t[k]


class KB:
    def __init__(s, nc):
        s.nc = nc
        s.eng = {'pe': nc.tensor, 'act': nc.scalar, 'dve': nc.vector, 'pool': nc.gpsimd, 'sp': nc.sync}
        s.sem = {e: Sem(nc.alloc_semaphore("s_" + e), "s_" + e) for e in ('pe', 'act', 'dve', 'pool')}
        s.bsem = Sem(nc.alloc_semaphore("s_bar"), "s_bar")
        s.ccsem = Sem(nc.alloc_semaphore("s_cc"), "s_cc")
        s.waited = {e: {} for e in s.eng}
        s.dpool = [Sem(nc.alloc_semaphore("d%d" % i), "d%d" % i, dma=True) for i in range(72)]
        s.dfree = list(s.dpool)
        s.live = []
        s.ninst = 0

    def _wait(s, e, evs):
        for (S, val) in evs:
            if S.dma:
                val = S.cnt
            if e == 'pe' and S is s.sem['pe']:
                continue
            if s.waited[e].get(S.key, 0) >= val:
                continue
            s.eng[e].wait_ge(S.h, val)
            s.waited[e][S.key] = val
            s.ninst += 1

    def _deps(s, reads, writes):
        evs = []
        for r in reads:
            if r.w is not None:
                evs.append(r.w)
            if getattr(r, "psum", False):
                evs.extend(r.r.values())
        for w in writes:
            if w.w is not None:
                evs.append(w.w)
            evs.extend(w.r.values())
        return evs

    @staticmethod
    def _res(xs):
        out = []
        for x in xs:
            out.append(x.R if isinstance(x, Tl) else x)
        return out

    def op(s, e, fn, reads=(), writes=()):
        reads = s._res(reads)
        writes = s._res(writes)
        s._wait(e, s._deps(reads, writes))
        ins = fn()
        S = s.sem[e]
        S.cnt += 1
        ins.then_inc(S.h, 1)
        s.ninst += 1
        ev = (S, S.cnt)
        for r in reads:
            r.r[S.key] = ev
        for w in writes:
            w.w = ev
            w.r = {}
        return ins

    def dma(s, q, out, in_, reads=(), writes=(), sem_of=None):
        reads = s._res(reads)
        writes = s._res(writes)
        s._wait(q, s._deps(reads, writes))
        ins = s.eng[q].dma_start(out=out, in_=in_)
        rr = sem_of.R if isinstance(sem_of, Tl) else sem_of
        if rr.dsem is None:
            rr.dsem = s.dfree.pop()
        D = rr.dsem
        D.cnt += 16
        ins.then_inc(D.h, 16)
        s.ninst += 1
        ev = (D, D.cnt)
        for r in reads:
            r.r[D.key] = ev
        for w in writes:
            w.w = ev
            w.r = {}
        return ins

    def barrier(s):
        evs = [(S, S.cnt) for S in s.sem.values()] + [(D, D.cnt) for D in s.dpool] + [(s.ccsem, s.ccsem.cnt)]
        s._wait('sp', evs)
        s.nc.sync.sem_inc(s.bsem.h, 1)
        s.bsem.cnt += 1
        for e in ('pe', 'act', 'dve', 'pool'):
            s.eng[e].wait_ge(s.bsem.h, s.bsem.cnt)
        for r in s.live:
            r.w = None
            r.r = {}

    def allgather(s, src, dst, nc_cores):
        s.barrier()
        s.nc.gpsimd.collective_compute("AllGather", ALU.bypass, replica_groups=[list(range(nc_cores))],
                                       ins=[src], outs=[dst]).then_inc(s.ccsem.h, 1)
        s.ccsem.cnt += 1
        s.nc.gpsimd.wait_ge(s.ccsem.h, s.ccsem.cnt)
        s.barrier()

    def sbt(s, stack, shape, dtype, name):
        s.ntile = getattr(s, "ntile", 0) + 1
        name = "sb_%s_%d" % (name, s.ntile)
        t = stack.enter_context(s.nc.sbuf_tensor(name, list(shape), dtype))
        tl = Tl(t, name)
        s.live.append(tl.R)
        tl._stack = stack
        return tl

    def end_phase(s, tiles):
        for tl in tiles:
            if tl.R.dsem is not None:
                s.dfree.append(tl.R.dsem)
                tl.R.dsem = None
            if tl.R in s.live:
                s.live.remove(tl.R)


class Phase:
    def __init__(s, kb):
        s.kb = kb
        s.stack = ExitStack()
        s.tiles = []

    def __enter__(s):
        s.stack.__enter__()
        return s

    def sbt(s, shape, dtype, name):
        tl = s.kb.sbt(s.stack, shape, dtype, name)
        s.tiles.append(tl)
        return tl

    def __exit__(s, *a):
        if a[0] is None:
            s.kb.barrier()
        s.kb.end_phase(s.tiles)
        return s.stack.__exit__(*a)


def lay_w(W):
    K, N = W.shape
    return np.ascontiguousarray(W.reshape(K // 128, 128, N).transpose(1, 0, 2))


def pvec(v, FC):
    lead = v.shape[:-1]
    x = v.reshape(lead + (FC, 128))
    x = np.moveaxis(x, -1, 0)
    return np.ascontiguousarray(x)


def prep_inputs(cfg, inp):
    NC, D, B, S, T = cfg.NC, cfg.D, cfg.B, cfg.S, cfg.T
    FPC, FC, HS, HG, KC = cfg.FPC, cfg.FC, cfg.HS, cfg.HG, cfg.KC
    f32 = np.float32
    g = {k: np.asarray(v) for k, v in inp.items()}
    x2 = g["x"].reshape(T, D)
    maps = []
    p = np.arange(128)
    consts = {}
    consts["ident"] = np.eye(128, dtype=f32)
    consts["ltri_le"] = (p[:, None] <= p[None, :]).astype(f32)
    consts["ugt"] = (p[:, None] > p[None, :]).astype(f32)
    t512 = np.arange(512)
    consts["sbmask"] = np.stack([((r * 128 + p)[:, None] < t512[None, :]).astype(f32) for r in range(4)], 1)
    consts["sgumask"] = (p[:, None] <= p[None, :]).astype(f32)
    consts["negmaskT"] = np.where(p[None, :] >= p[:, None], 0.0, NEG).astype(f32)
    consts["strictT"] = (p[None, :] > p[:, None]).astype(f32)
    lv = []
    for l in range(7):
        b = 1 << l
        i = p[:, None]
        j = p[None, :]
        lv.append(((i // (2 * b) == j // (2 * b)) & ((i // b) % 2 == 1) & ((j // b) % 2 == 0)).astype(f32))
    consts["lvmask"] = np.stack(lv, 1)
    for c in range(NC):
        m = dict(consts)
        fs = slice(c * FPC, (c + 1) * FPC)
        m["xT"] = np.ascontiguousarray(x2[:, fs].T)
        m["cT"] = np.ascontiguousarray(g["c"].T.reshape(KC, 128, B).transpose(1, 0, 2))
        m["adaw"] = np.stack([lay_w(g["ada_w"][:, mm * D + c * FPC: mm * D + (c + 1) * FPC]) for mm in range(6)], 0)
        m["adab"] = pvec(g["ada_b"].reshape(6, D)[:, fs], FC)
        m["adal"] = pvec(g["ada_layer"][:, :, fs], FC)
        m["nmix"] = pvec(g["norm_mix"][:, fs], FC)
        m["nffn"] = pvec(g["norm_ffn"][:, fs], FC)
        m["nfin"] = pvec(g["norm_final"][fs], FC)
        h0 = c * HS * 128
        hw = HS * 128
        evin, evout, sgug, sguw, sgub = [], [], [], [], []
        for e in range(cfg.NE):
            W = g["ev_w_in"][e]
            segs = [W[:, sgi * (D // 2) + h0: sgi * (D // 2) + h0 + hw] for sgi in (0, 1, 3, 2, 4)]
            evin.append(lay_w(np.concatenate(segs, 1)))
            Wo = g["ev_w_out"][e]
            rows = []
            for r in range(NC):
                for j in range(2 * HS):
                    if j < HS:
                        base = (r * HS + j) * 128
                    else:
                        base = D // 2 + (r * HS + j - HS) * 128
                    rows.append(Wo[base:base + 128, fs])
            evout.append(lay_w(np.concatenate(rows, 0)))
            sgug.append(np.broadcast_to(g["sgu_gain"][e, c * HS:(c + 1) * HS].reshape(1, hw), (128, hw)))
            sguw.append(np.stack([g["sgu_w"][e, c * HS + j].T for j in range(HS)], 1))
            sgub.append(np.broadcast_to(g["sgu_b"][e, c * HS:(c + 1) * HS].reshape(1, HS, 128), (128, HS, 128)))
        m["evin"] = np.ascontiguousarray(np.stack(evin, 0))
        m["evout"] = np.ascontiguousarray(np.stack(evout, 0))
        m["sgug"] = np.ascontiguousarray(np.stack(sgug, 0))
        m["sguw"] = np.ascontiguousarray(np.stack(sguw, 0))
        m["sgub"] = np.ascontiguousarray(np.stack(sgub, 0))
        if cfg.NO > 0:
            H = D // 128
            g0 = c * HG * 128
            gw = HG * 128
            gin, gab, gcw, gal, gdt, gog, gout = [], [], [], [], [], [], []
            for o in range(cfg.NO):
                W = g["gdn_w_in"][o]
                segs = [W[:, sgi * D + g0: sgi * D + g0 + gw] for sgi in range(4)]
                gin.append(lay_w(np.concatenate(segs, 1)))
                gab.append(lay_w(np.concatenate([W[:, 4 * D + c * HG: 4 * D + (c + 1) * HG],
                                                  W[:, 4 * D + H + c * HG: 4 * D + H + (c + 1) * HG]], 1)))
                cw = g["gdn_conv"][o]
                cws = np.stack([cw[:, sgi * D + g0: sgi * D + g0 + gw] for sgi in range(3)], 0)
                cws = cws.reshape(3, 4, HG, 128).transpose(3, 0, 2, 1)
                gcw.append(cws)
                gal.append(np.broadcast_to(g["gdn_a_log"][o, c * HG:(c + 1) * HG].reshape(1, HG), (128, HG)))
                gdt.append(np.broadcast_to(g["gdn_dt_bias"][o, c * HG:(c + 1) * HG].reshape(1, HG), (128, HG)))
                gog.append(g["gdn_o_gain"][o].reshape(128, 1))
                gout.append(lay_w(g["gdn_w_out"][o][:, fs]))
            m["gin"] = np.ascontiguousarray(np.stack(gin, 0))
            m["gab"] = np.ascontiguousarray(np.stack(gab, 0))
            m["gcw"] = np.ascontiguousarray(np.stack(gcw, 0))
            m["gal"] = np.ascontiguousarray(np.stack(gal, 0))
            m["gdt"] = np.ascontiguousarray(np.stack(gdt, 0))
            m["gog"] = np.ascontiguousarray(np.stack(gog, 0))
            m["gout"] = np.ascontiguousarray(np.stack(gout, 0))
        FFPC, FFC, DFF = cfg.FFPC, cfg.FFC, cfg.DFF
        fup, fcw, fcb, fdn = [], [], [], []
        for l in range(cfg.DEPTH):
            W = g["ffn_w_up"][l]
            fup.append(lay_w(np.concatenate([W[:, c * FFPC:(c + 1) * FFPC], W[:, DFF + c * FFPC: DFF + (c + 1) * FFPC]], 1)))
            fcw.append(g["ffn_conv"][l][:, c * FFPC:(c + 1) * FFPC].reshape(3, FFC, 128).transpose(2, 1, 0))
            fcb.append(g["ffn_conv_b"][l][c * FFPC:(c + 1) * FFPC].reshape(FFC, 128).T)
            fdn.append(lay_w(g["ffn_w_down"][l][:, fs]))
        m["fup"] = np.ascontiguousarray(np.stack(fup, 0))
        m["fcw"] = np.ascontiguousarray(np.stack(fcw, 0))
        m["fcb"] = np.ascontiguousarray(np.stack(fcb, 0))
        m["fdn"] = np.ascontiguousarray(np.stack(fdn, 0))
        maps.append({k: np.ascontiguousarray(v, dtype=f32) for k, v in m.items()})
    return maps


def build(cfg, dbg=False):
    c = cfg
    NC, D, B, S, T, KC, FPC, FC, HS, HG = c.NC, c.D, c.B, c.S, c.T, c.KC, c.FPC, c.FC, c.HS, c.HG
    FFPC, FFC, KC2, TT, NT, DEPTH = c.FFPC, c.FFC, c.KC2, c.TT, c.NT, c.DEPTH
    hw = HS * 128
    gw = HG * 128
    nc = bass.Bass("TRN2", target_bir_lowering=False)
    kb = KB(nc)

    def din(name, shape):
        return nc.dram_tensor(name, list(shape), F32, kind="ExternalInput")

    def dint(name, shape, dt, ext=False):
        if ext and dbg:
            return nc.dram_tensor(name, list(shape), dt, kind="ExternalOutput")
        return nc.dram_tensor(name, list(shape), dt)

    I = {}
    I["xT"] = din("xT", [FPC, T]); I["cT"] = din("cT", [128, KC, B]); I["adaw"] = din("adaw", [6, 128, KC, FPC])
    I["adab"] = din("adab", [128, 6, FC]); I["adal"] = din("adal", [128, DEPTH, 6, FC])
    I["nmix"] = din("nmix", [128, DEPTH, FC]); I["nffn"] = din("nffn", [128, DEPTH, FC]); I["nfin"] = din("nfin", [128, FC])
    I["evin"] = din("evin", [c.NE, 128, KC, 5 * hw]); I["evout"] = din("evout", [c.NE, 128, KC, FPC])
    I["sgug"] = din("sgug", [c.NE, 128, hw]); I["sguw"] = din("sguw", [c.NE, 128, HS, 128]); I["sgub"] = din("sgub", [c.NE, 128, HS, 128])
    if c.NO > 0:
        I["gin"] = din("gin", [c.NO, 128, KC, 4 * gw]); I["gab"] = din("gab", [c.NO, 128, KC, 2 * HG])
        I["gcw"] = din("gcw", [c.NO, 128, 3, HG, 4]); I["gal"] = din("gal", [c.NO, 128, HG]); I["gdt"] = din("gdt", [c.NO, 128, HG])
        I["gog"] = din("gog", [c.NO, 128, 1]); I["gout"] = din("gout", [c.NO, 128, KC, FPC])
    I["fup"] = din("fup", [DEPTH, 128, KC, 2 * FFPC]); I["fcw"] = din("fcw", [DEPTH, 128, FFC, 3]); I["fcb"] = din("fcb", [DEPTH, 128, FFC])
    I["fdn"] = din("fdn", [DEPTH, 128, KC2, FPC])
    for nm, shp in (("ident", [128, 128]), ("ltri_le", [128, 128]), ("ugt", [128, 128]), ("sbmask", [128, 4, 512]),
                    ("sgumask", [128, 128]), ("negmaskT", [128, 128]), ("strictT", [128, 128]), ("lvmask", [128, 7, 128])):
        I[nm] = din(nm, shp)
    yT = nc.dram_tensor("yT", [FPC, T], F32, kind="ExternalOutput")

    XT = dint("XT", [FPC, T], F32)
    ssP = dint("ssP", [1, T], F32); ssA = dint("ssA", [NC, T], F32)
    hP = dint("hP", [FPC, T], BF16); hA = dint("hA", [D, T], BF16)
    aP = dint("aP", [FFPC, T], BF16); aA = dint("aA", [c.DFF, T], BF16)
    qT = dint("qT", [hw, T], BF16, True); kT = dint("kT", [hw, T], BF16, True); uT = dint("uT", [hw, T], BF16, True)
    v_tm = dint("v_tm", [T, hw], BF16, True); vg_tm = dint("vg_tm", [T, hw], BF16, True)
    if dbg:
        dbg_x = nc.dram_tensor("dbg_x", [DEPTH * 2, FPC, T], F32, kind="ExternalOutput")
        dbg_o = nc.dram_tensor("dbg_o", [DEPTH, FPC, T], BF16, kind="ExternalOutput")
        dbg_h = nc.dram_tensor("dbg_h", [DEPTH * 2, FPC, T], BF16, kind="ExternalOutput")

    def fview(d):
        return d.ap().rearrange("(fc p) t -> p fc t", p=128)

    gstack = ExitStack()
    with gstack, nc.Block() as block:
        @block.sync
        def _(sync_eng):
            ps = []
            for i in range(8):
                t = gstack.enter_context(nc.psum_tensor("ps%d" % i, [128, 512], F32))
                ps.append(Tl(t, "ps%d" % i))
                ps[-1].R.psum = True
                kb.live.append(ps[-1].R)

            def gtile(shape, dt, name):
                return kb.sbt(gstack, shape, dt, name)

            MOD = gtile([128, DEPTH * B * 6 * FC], F32, "MOD")
            AM = gtile([128, DEPTH * B * FC], F32, "AM")
            AFN = gtile([128, DEPTH * B * FC], F32, "AFN")
            nfin = gtile([128, FC], F32, "nfin")
            ident = gtile([128, 128], F32, "ident")
            epsc = gtile([128, 1], F32, "epsc")
            kb.op('pool', lambda: nc.gpsimd.memset(epsc[:], EPS), writes=[epsc])
            dscr = R("dscr")
            kb.live.append(dscr)

            def midx(l, b, m, fc):
                return ((l * B + b) * 6 + m) * FC + fc

            def mcol(l, b, m, fc):
                i = midx(l, b, m, fc)
                return MOD.t[:, i:i + 1]

            with Phase(kb) as ph:
                nsp = 8 if T >= 4096 else 1
                for i in range(nsp):
                    w = T // nsp
                    kb.dma('sp', out=XT[:, i * w:(i + 1) * w], in_=I["xT"][:, i * w:(i + 1) * w], sem_of=dscr)
                kb.dma('sp', out=nfin[:], in_=I["nfin"][:, :], writes=[nfin], sem_of=nfin)
                kb.dma('sp', out=ident[:], in_=I["ident"][:, :], writes=[ident], sem_of=ident)
                cT = ph.sbt([128, KC * B], F32, "cT")
                scT = ph.sbt([128, KC * B], F32, "scT")
                adab = ph.sbt([128, 6 * FC], F32, "adab")
                adal = ph.sbt([128, DEPTH * 6 * FC], F32, "adal")
                nmix = ph.sbt([128, DEPTH * FC], F32, "nmix")
                nffn = ph.sbt([128, DEPTH * FC], F32, "nffn")
                modall = ph.sbt([128, B * 6 * FC], F32, "modall")
                kb.dma('sp', out=cT[:], in_=I["cT"].ap().rearrange("p k b -> p (k b)"), writes=[cT], sem_of=cT)
                kb.dma('sp', out=adab[:], in_=I["adab"].ap().rearrange("p a b -> p (a b)"), writes=[adab], sem_of=adab)
                kb.dma('sp', out=adal[:], in_=I["adal"].ap().rearrange("p l a b -> p (l a b)"), writes=[adal], sem_of=adal)
                kb.dma('sp', out=nmix[:], in_=I["nmix"].ap().rearrange("p l b -> p (l b)"), writes=[nmix], sem_of=nmix)
                kb.dma('sp', out=nffn[:], in_=I["nffn"].ap().rearrange("p l b -> p (l b)"), writes=[nffn], sem_of=nffn)
                kb.op('act', lambda: nc.scalar.activation(out=scT[:], in_=cT[:], func=AF.Silu), reads=[cT], writes=[scT])
                wts = [ph.sbt([128, KC, FPC], F32, "adaw%d" % i) for i in range(2)]
                mv = modall.t[:].rearrange("p (b n) -> p b n", b=B)
                for m in range(6):
                    wt = wts[m % 2]
                    kb.dma('sp', out=wt[:], in_=I["adaw"][m], writes=[wt], sem_of=wt)
                    for fc in range(FC):
                        pb = ps[(m * FC + fc) % 4]
                        for kc in range(KC):
                            kb.op('pe', lambda pb=pb, wt=wt, kc=kc, fc=fc: nc.tensor.matmul(
                                pb[:, 0:B], lhsT=wt[:, kc, fc * 128:(fc + 1) * 128], rhs=scT[:, kc * B:(kc + 1) * B],
                                start=(kc == 0), stop=(kc == KC - 1)), reads=[wt, scT], writes=[pb])
                        n = m * FC + fc
                        kb.op('dve', lambda pb=pb, n=n: nc.vector.tensor_scalar(
                            out=mv[:, :, n], in0=pb[:, 0:B], scalar1=adab[:, n:n + 1], scalar2=None, op0=ALU.add),
                            reads=[pb, adab], writes=[modall])
                for l in range(DEPTH):
                    for b in range(B):
                        i0 = midx(l, b, 0, 0)
                        kb.op('dve', lambda i0=i0, l=l, b=b: nc.vector.tensor_tensor(
                            out=MOD[:, i0:i0 + 6 * FC], in0=modall[:, b * 6 * FC:(b + 1) * 6 * FC],
                            in1=adal[:, l * 6 * FC:(l + 1) * 6 * FC], op=ALU.add), reads=[modall, adal], writes=[MOD])
                        for (dst, mm_, nrm) in ((AM, 1, nmix), (AFN, 4, nffn)):
                            i1 = midx(l, b, mm_, 0)
                            kb.op('dve', lambda dst=dst, i1=i1, nrm=nrm, l=l, b=b: nc.vector.scalar_tensor_tensor(
                                out=dst[:, (l * B + b) * FC:(l * B + b + 1) * FC], in0=MOD[:, i1:i1 + FC], scalar=1.0,
                                op0=ALU.add, in1=nrm[:, l * FC:(l + 1) * FC], op1=ALU.mult), reads=[MOD, nrm], writes=[dst])

            def ss_phase():
                with Phase(kb) as ph:
                    ones_c = ph.sbt([128, 1], F32, "ones_c")
                    kb.op('pool', lambda: nc.gpsimd.memset(ones_c[:], 1.0), writes=[ones_c])
                    xb = [ph.sbt([128, FC, TT], F32, "ssx%d" % i) for i in range(2)]
                    sq = [ph.sbt([128, FC, TT], F32, "sssq%d" % i) for i in range(2)]
                    so = [ph.sbt([1, TT], F32, "sso%d" % i) for i in range(2)]
                    def ld(tt):
                        kb.dma('sp', out=xb[tt % 2][:], in_=fview(XT)[:, :, tt * TT:(tt + 1) * TT], writes=[xb[tt % 2]], sem_of=xb[tt % 2])
                    ld(0)
                    for tt in range(NT):
                        k = tt % 2
                        tsl = slice(tt * TT, (tt + 1) * TT)
                        if tt + 1 < NT:
                            ld(tt + 1)
                        kb.op('act', lambda k=k: nc.scalar.activation(out=sq[k][:], in_=xb[k][:], func=AF.Square),
                              reads=[xb[k]], writes=[sq[k]])
                        pb = ps[k]
                        for fc in range(FC):
                            kb.op('pe', lambda k=k, fc=fc, pb=pb: nc.tensor.matmul(
                                pb[0:1, :], lhsT=ones_c[:, 0:1], rhs=sq[k][:, fc, :], start=(fc == 0), stop=(fc == FC - 1)),
                                reads=[ones_c, sq[k]], writes=[pb])
                        kb.op('dve', lambda k=k, pb=pb: nc.vector.tensor_copy(out=so[k][:], in_=pb[0:1, :]),
                              reads=[pb], writes=[so[k]])
                        kb.dma('sp', out=ssP[0:1, tsl], in_=so[k][:], reads=[so[k]], sem_of=so[k])
                kb.allgather(ssP.ap(), ssA.ap(), NC)

            def norm_phase(scale_col, bias_col, out_d, out_dt, dbg_dst=None):
                with Phase(kb) as ph:
                    onesN = ph.sbt([NC, 128], F32, "onesN")
                    kb.op('pool', lambda: nc.gpsimd.memset(onesN[:], 1.0), writes=[onesN])
                    ssa = [ph.sbt([NC, TT], F32, "ssa%d" % i) for i in range(2)]
                    rs = [ph.sbt([128, TT], F32, "rs%d" % i) for i in range(2)]
                    xb = [ph.sbt([128, FC, TT], F32, "nx%d" % i) for i in range(2)]
                    tmp = [ph.sbt([128, TT], F32, "ntmp%d" % i) for i in range(2)]
                    ob = [ph.sbt([128, FC, TT], out_dt, "nob%d" % i) for i in range(2)]
                    def ld(tt):
                        k_ = tt % 2
                        ts_ = slice(tt * TT, (tt + 1) * TT)
                        kb.dma('sp', out=ssa[k_][:], in_=ssA[:, ts_], writes=[ssa[k_]], sem_of=ssa[k_])
                        kb.dma('sp', out=xb[k_][:], in_=fview(XT)[:, :, ts_], writes=[xb[k_]], sem_of=xb[k_])
                    ld(0)
                    for tt in range(NT):
                        k = tt % 2
                        b = (tt * TT) // S
                        tsl = slice(tt * TT, (tt + 1) * TT)
                        if tt + 1 < NT:
                            ld(tt + 1)
                        pb = ps[k]
                        kb.op('pe', lambda k=k, pb=pb: nc.tensor.matmul(pb[:, :], lhsT=onesN[:, :], rhs=ssa[k][:, :],
                                                                        start=True, stop=True), reads=[onesN, ssa[k]], writes=[pb])
                        kb.op('act', lambda k=k, pb=pb: nc.scalar.activation(
                            out=rs[k][:], in_=pb[:, :], func=AF.Sqrt, scale=1.0 / D, bias=epsc[:, 0:1]),
                            reads=[pb, epsc], writes=[rs[k]])
                        kb.op('dve', lambda k=k: nc.vector.reciprocal(out=rs[k][:], in_=rs[k][:]), reads=[rs[k]], writes=[rs[k]])
                        for fc in range(FC):
                            tk = tmp[fc % 2]
                            kb.op('dve', lambda k=k, fc=fc, tk=tk: nc.vector.tensor_tensor(
                                out=tk[:], in0=xb[k][:, fc, :], in1=rs[k][:], op=ALU.mult), reads=[xb[k], rs[k]], writes=[tk])
                            bc = bias_col(b, fc)
                            if bc is None:
                                kb.op('act', lambda k=k, fc=fc, tk=tk, b=b: nc.scalar.activation(
                                    out=ob[k][:, fc, :], in_=tk[:], func=AF.Copy, scale=scale_col(b, fc)),
                                    reads=[tk], writes=[ob[k]])
                            else:
                                kb.op('act', lambda k=k, fc=fc, tk=tk, b=b, bc=bc: nc.scalar.activation(
                                    out=ob[k][:, fc, :], in_=tk[:], func=AF.Identity, scale=scale_col(b, fc), bias=bc),
                                    reads=[tk], writes=[ob[k]])
                        kb.dma('sp', out=fview(out_d)[:, :, tsl], in_=ob[k][:], reads=[ob[k]], sem_of=ob[k])
                        if dbg_dst is not None:
                            kb.dma('sp', out=dbg_dst.rearrange("(fc p) t -> p fc t", p=128)[:, :, tsl], in_=ob[k][:],
                                   reads=[ob[k]], sem_of=ob[k])

            def load_w(ph, src, kcn, ncols, name, col_slices=None):
                tiles = []
                for g0 in range(0, kcn, 4):
                    gl = min(4, kcn - g0)
                    wt = ph.sbt([128, gl, ncols], BF16, "%s%d" % (name, g0))
                    if col_slices is None:
                        kb.dma('pool', out=wt[:], in_=src[:, g0:g0 + gl, :], writes=[wt], sem_of=wt)
                    else:
                        o = 0
                        for (a, bnd) in col_slices:
                            kb.dma('pool', out=wt[:, :, o:o + (bnd - a)], in_=src[:, g0:g0 + gl, a:bnd], writes=[wt], sem_of=wt)
                            o += bnd - a
                    tiles.append(wt)
                return tiles

            def proj_res_phase(wsrc, src_d, kcn, l, m, TL, dbg_dst=None):
                with Phase(kb) as ph:
                    W = load_w(ph, wsrc, kcn, FPC, "Wpr")
                    ab = [ph.sbt([128, kcn, TL], BF16, "prab%d" % i) for i in range(2)]
                    xb = [ph.sbt([128, FC, TL], F32, "prx%d" % i) for i in range(2)]
                    sv = src_d.ap().rearrange("(kc p) t -> p kc t", p=128)
                    def ld(tt):
                        k_ = tt % 2
                        ts_ = slice(tt * TL, (tt + 1) * TL)
                        kb.dma('sp', out=ab[k_][:], in_=sv[:, :, ts_], writes=[ab[k_]], sem_of=ab[k_])
                        kb.dma('sp', out=xb[k_][:], in_=fview(XT)[:, :, ts_], writes=[xb[k_]], sem_of=xb[k_])
                    ld(0)
                    for tt in range(T // TL):
                        k = tt % 2
                        b = (tt * TL) // S
                        tsl = slice(tt * TL, (tt + 1) * TL)
                        if tt + 1 < T // TL:
                            ld(tt + 1)
                        for fc in range(FC):
                            pb = ps[fc % 4]
                            for kc in range(kcn):
                                kb.op('pe', lambda pb=pb, kc=kc, fc=fc, k=k: nc.tensor.matmul(
                                    pb[:, 0:TL], lhsT=W[kc // 4][:, kc % 4, fc * 128:(fc + 1) * 128], rhs=ab[k][:, kc, :],
                                    start=(kc == 0), stop=(kc == kcn - 1)), reads=[W[kc // 4], ab[k]], writes=[pb])
                            kb.op('dve', lambda pb=pb, fc=fc, k=k, b=b: nc.vector.scalar_tensor_tensor(
                                out=xb[k][:, fc, :], in0=pb[:, 0:TL], scalar=mcol(l, b, m, fc), op0=ALU.mult,
                                in1=xb[k][:, fc, :], op1=ALU.add), reads=[pb, xb[k], MOD], writes=[xb[k]])
                        kb.dma('sp', out=fview(XT)[:, :, tsl], in_=xb[k][:], reads=[xb[k]], sem_of=xb[k])
                        if dbg_dst is not None:
                            kb.dma('sp', out=dbg_dst.rearrange("(fc p) t -> p fc t", p=128)[:, :, tsl], in_=xb[k][:],
                                   reads=[xb[k]], sem_of=xb[k])

            def even_inproj(e):
                with Phase(kb) as ph:
                    W = load_w(ph, I["evin"][e], KC, 5 * hw, "Wev")
                    sgug = ph.sbt([128, hw], F32, "sgug")
                    kb.dma('sp', out=sgug[:], in_=I["sgug"][e], writes=[sgug], sem_of=sgug)
                    hb = [ph.sbt([128, KC, TT], BF16, "evh%d" % i) for i in range(2)]
                    obf = [ph.sbt([128, TT], BF16, "evo%d" % i) for i in range(2)]
                    vb = [ph.sbt([128, hw], BF16, "evv%d" % i) for i in range(2)]
                    gt = [ph.sbt([128, hw], F32, "evg%d" % i) for i in range(2)]
                    junk = [ph.sbt([128, 128], F32, "evj%d" % i) for i in range(2)]
                    ssq = [ph.sbt([128, HS], F32, "evss%d" % i) for i in range(2)]
                    vgn = [ph.sbt([128, hw], BF16, "evn%d" % i) for i in range(2)]
                    hv = hA.ap().rearrange("(kc p) t -> p kc t", p=128)
                    dsts = [qT] * HS + [kT] * HS + [uT] * HS
                    u = 0
                    def ld(tt):
                        kb.dma('sp', out=hb[tt % 2][:], in_=hv[:, :, tt * TT:(tt + 1) * TT], writes=[hb[tt % 2]], sem_of=hb[tt % 2])
                    ld(0)
                    for tt in range(NT):
                        k = tt % 2
                        tsl = slice(tt * TT, (tt + 1) * TT)
                        if tt + 1 < NT:
                            ld(tt + 1)
                        for ci in range(3 * HS):
                            pb = ps[ci % 4]
                            for kc in range(KC):
                                kb.op('pe', lambda pb=pb, kc=kc, ci=ci, k=k: nc.tensor.matmul(
                                    pb[:, :], lhsT=W[kc // 4][:, kc % 4, ci * 128:(ci + 1) * 128], rhs=hb[k][:, kc, :],
                                    start=(kc == 0), stop=(kc == KC - 1)), reads=[W[kc // 4], hb[k]], writes=[pb])
                            ob = obf[ci % 2]
                            fn = AF.Gelu_apprx_tanh if ci >= 2 * HS else AF.Copy
                            kb.op('act', lambda pb=pb, ob=ob, fn=fn: nc.scalar.activation(out=ob[:], in_=pb[:, :], func=fn),
                                  reads=[pb], writes=[ob])
                            r0 = (ci % HS) * 128
                            kb.dma('sp', out=dsts[ci][r0:r0 + 128, tsl], in_=ob[:], reads=[ob], sem_of=ob)
                        for sub in range(TT // 128):
                            k2 = u % 2
                            u += 1
                            pb = ps[4 + k2]
                            for kc in range(KC):
                                kb.op('pe', lambda pb=pb, kc=kc, sub=sub, k=k: nc.tensor.matmul(
                                    pb[:, 0:2 * hw], lhsT=hb[k][:, kc, sub * 128:(sub + 1) * 128],
                                    rhs=W[kc // 4][:, kc % 4, 3 * hw:5 * hw], start=(kc == 0), stop=(kc == KC - 1)),
                                    reads=[W[kc // 4], hb[k]], writes=[pb])
                            rows = slice(tt * TT + sub * 128, tt * TT + (sub + 1) * 128)
                            kb.op('act', lambda pb=pb, k2=k2: nc.scalar.activation(out=vb[k2][:], in_=pb[:, 0:hw], func=AF.Copy),
                                  reads=[pb], writes=[vb[k2]])
                            kb.dma('sp', out=v_tm[rows, :], in_=vb[k2][:], reads=[vb[k2]], sem_of=vb[k2])
                            kb.op('act', lambda pb=pb, k2=k2: nc.scalar.activation(out=gt[k2][:], in_=pb[:, hw:2 * hw],
                                                                                   func=AF.Gelu_apprx_tanh),
                                  reads=[pb], writes=[gt[k2]])
                            for g in range(HS):
                                kb.op('act', lambda k2=k2, g=g: nc.scalar.activation(
                                    out=junk[k2][:], in_=gt[k2][:, g * 128:(g + 1) * 128], func=AF.Square,
                                    accum_out=ssq[k2][:, g:g + 1]), reads=[gt[k2]], writes=[junk[k2], ssq[k2]])
                            kb.op('act', lambda k2=k2: nc.scalar.activation(
                                out=ssq[k2][:], in_=ssq[k2][:], func=AF.Sqrt, scale=1.0 / 128, bias=epsc[:, 0:1]),
                                reads=[ssq[k2], epsc], writes=[ssq[k2]])
                            kb.op('dve', lambda k2=k2: nc.vector.reciprocal(out=ssq[k2][:], in_=ssq[k2][:]),
                                  reads=[ssq[k2]], writes=[ssq[k2]])
                            for g in range(HS):
                                kb.op('dve', lambda k2=k2, g=g: nc.vector.scalar_tensor_tensor(
                                    out=vgn[k2][:, g * 128:(g + 1) * 128], in0=gt[k2][:, g * 128:(g + 1) * 128],
                                    scalar=ssq[k2][:, g:g + 1], op0=ALU.mult, in1=sgug[:, g * 128:(g + 1) * 128], op1=ALU.mult),
                                    reads=[gt[k2], ssq[k2], sgug], writes=[vgn[k2]])
                            kb.dma('sp', out=vg_tm[rows, :], in_=vgn[k2][:], reads=[vgn[k2]], sem_of=vgn[k2])

            def sb_attention():
                SC = 128.0 ** -0.5
                NB = S // 128
                with Phase(kb) as ph:
                    Uf = ph.sbt([128, 128], BF16, "Uf")
                    kb.dma('pool', out=Uf[:], in_=I["ugt"][:, :], writes=[Uf], sem_of=Uf)
                    onesb = ph.sbt([128, 128], BF16, "onesb")
                    kb.op('pool', lambda: nc.gpsimd.memset(onesb[:], 1.0), writes=[onesb])
                    Mk = ph.sbt([128, 4, 512], BF16, "Mk")
                    kb.dma('pool', out=Mk[:], in_=I["sbmask"][:, :, :], writes=[Mk], sem_of=Mk)
                    qs = [ph.sbt([128, S], BF16, "qs%d" % i) for i in range(2)]
                    ks = [ph.sbt([128, S], BF16, "ks%d" % i) for i in range(2)]
                    vs = [ph.sbt([128, NB, 128], BF16, "vs%d" % i) for i in range(2)]
                    et = [ph.sbt([128, 512], F32, "et%d" % i) for i in range(2)]
                    spt = [ph.sbt([128, 512], BF16, "spt%d" % i) for i in range(2)]
                    spm = [ph.sbt([128, 512], BF16, "spm%d" % i) for i in range(2)]
                    t1 = [ph.sbt([128, 512], F32, "t1%d" % i) for i in range(2)]
                    t2 = [ph.sbt([128, 512], F32, "t2%d" % i) for i in range(2)]
                    wt = [ph.sbt([128, 512], BF16, "wt%d" % i) for i in range(2)]
                    wm = [ph.sbt([128, 512], BF16, "wm%d" % i) for i in range(2)]
                    Lacc = ph.sbt([128, 512], BF16, "Lacc")
                    ob = [ph.sbt([128, 512], BF16, "sbo%d" % i) for i in range(2)]
                    it = 0
                    u = 0
                    for b in range(B):
                        for hd in range(HS):
                            kk = it % 2
                            it += 1
                            kb.dma('sp', out=qs[kk][:], in_=qT[hd * 128:(hd + 1) * 128, b * S:(b + 1) * S], writes=[qs[kk]], sem_of=qs[kk])
                            kb.dma('sp', out=ks[kk][:], in_=kT[hd * 128:(hd + 1) * 128, b * S:(b + 1) * S], writes=[ks[kk]], sem_of=ks[kk])
                            kb.dma('sp', out=vs[kk][:], in_=v_tm[b * S:(b + 1) * S, hd * 128:(hd + 1) * 128].rearrange(
                                "(n p) d -> p n d", p=128), writes=[vs[kk]], sem_of=vs[kk])
                            for qt in range(S // 512):
                                oT = ps[4 + qt % 2]
                                first = True
                                for kbk in range(4 * qt + 3, -1, -1):
                                    r = kbk - 4 * qt
                                    j = u % 2
                                    u += 1
                                    zp = ps[j]
                                    tl = ps[2 + j]
                                    kb.op('pe', lambda zp=zp, kk=kk, kbk=kbk, qt=qt: nc.tensor.matmul(
                                        zp[:, :], lhsT=ks[kk][:, kbk * 128:(kbk + 1) * 128], rhs=qs[kk][:, qt * 512:(qt + 1) * 512],
                                        start=True, stop=True), reads=[ks[kk], qs[kk]], writes=[zp])
                                    kb.op('act', lambda zp=zp, j=j: nc.scalar.activation(out=et[j][:], in_=zp[:, :], func=AF.Exp, scale=SC),
                                          reads=[zp], writes=[et[j]])
                                    kb.op('act', lambda j=j: nc.scalar.activation(out=spt[j][:], in_=et[j][:], func=AF.Ln, bias=1.0),
                                          reads=[et[j]], writes=[spt[j]])
                                    if r >= 0:
                                        kb.op('pool', lambda j=j, r=r: nc.gpsimd.tensor_tensor(
                                            out=spm[j][:], in0=spt[j][:], in1=Mk[:, r, :], op=ALU.mult),
                                            reads=[spt[j], Mk], writes=[spm[j]])
                                        sp_ = spm[j]
                                    else:
                                        sp_ = spt[j]
                                    kb.op('pe', lambda tl=tl, sp_=sp_, first=first: nc.tensor.matmul(
                                        tl[:, :], lhsT=Uf[:], rhs=sp_[:], start=True, stop=first), reads=[Uf, sp_], writes=[tl])
                                    if not first:
                                        kb.op('pe', lambda tl=tl: nc.tensor.matmul(
                                            tl[:, :], lhsT=onesb[:], rhs=Lacc[:], start=False, stop=True), reads=[onesb, Lacc], writes=[tl])
                                    kb.op('dve', lambda zp=zp, j=j, sp_=sp_: nc.vector.scalar_tensor_tensor(
                                        out=t1[j][:], in0=zp[:, :], scalar=SC, op0=ALU.mult, in1=sp_[:], op1=ALU.subtract),
                                        reads=[zp, sp_], writes=[t1[j]])
                                    kb.op('dve', lambda tl=tl, j=j: nc.vector.tensor_tensor(
                                        out=t2[j][:], in0=t1[j][:], in1=tl[:, :], op=ALU.subtract), reads=[t1[j], tl], writes=[t2[j]])
                                    kb.op('act', lambda j=j: nc.scalar.activation(out=wt[j][:], in_=t2[j][:], func=AF.Exp),
                                          reads=[t2[j]], writes=[wt[j]])
                                    if r >= 0:
                                        kb.op('pool', lambda j=j, r=r: nc.gpsimd.tensor_tensor(
                                            out=wm[j][:], in0=wt[j][:], in1=Mk[:, r, :], op=ALU.mult),
                                            reads=[wt[j], Mk], writes=[wm[j]])
                                        w_ = wm[j]
                                    else:
                                        w_ = wt[j]
                                    kb.op('pe', lambda oT=oT, kk=kk, kbk=kbk, w_=w_, first=first: nc.tensor.matmul(
                                        oT[:, :], lhsT=vs[kk][:, kbk, :], rhs=w_[:], start=first, stop=(kbk == 0)),
                                        reads=[vs[kk], w_], writes=[oT])
                                    if kbk > 0:
                                        if first:
                                            kb.op('pool', lambda sp_=sp_: nc.gpsimd.tensor_copy(out=Lacc[:], in_=sp_[:]),
                                                  reads=[sp_], writes=[Lacc])
                                        else:
                                            kb.op('pool', lambda sp_=sp_: nc.gpsimd.tensor_tensor(
                                                out=Lacc[:], in0=Lacc[:], in1=sp_[:], op=ALU.add), reads=[sp_, Lacc], writes=[Lacc])
                                    first = False
                                o_ = ob[qt % 2]
                                kb.op('act', lambda oT=oT, o_=o_: nc.scalar.activation(out=o_[:], in_=oT[:, :], func=AF.Copy),
                                      reads=[oT], writes=[o_])
                                kb.dma('sp', out=hP[hd * 128:(hd + 1) * 128, b * S + qt * 512: b * S + (qt + 1) * 512], in_=o_[:],
                                       reads=[o_], sem_of=o_)

            def sgu(e):
                with Phase(kb) as ph:
                    wf = ph.sbt([128, HS, 128], F32, "sguwf")
                    msk = ph.sbt([128, 128], F32, "sgumsk")
                    wb = ph.sbt([128, HS, 128], BF16, "sguwb")
                    bb = ph.sbt([128, HS, 128], F32, "sgubb")
                    kb.dma('sp', out=wf[:], in_=I["sguw"][e], writes=[wf], sem_of=wf)
                    kb.dma('sp', out=msk[:], in_=I["sgumask"][:, :], writes=[msk], sem_of=msk)
                    kb.dma('sp', out=bb[:], in_=I["sgub"][e], writes=[bb], sem_of=bb)
                    for g in range(HS):
                        kb.op('dve', lambda g=g: nc.vector.tensor_tensor(out=wb[:, g, :], in0=wf[:, g, :], in1=msk[:], op=ALU.mult),
                              reads=[wf, msk], writes=[wb])
                    vgs = [ph.sbt([128, 4, 128], BF16, "sgv%d" % i) for i in range(2)]
                    us = [ph.sbt([128, 512], BF16, "sgu%d" % i) for i in range(2)]
                    tm = [ph.sbt([128, 512], F32, "sgt%d" % i) for i in range(2)]
                    ob = [ph.sbt([128, 512], BF16, "sgo%d" % i) for i in range(2)]
                    u = 0
                    its = [(g, tt) for g in range(HS) for tt in range(NT)]

                    def ld(i):
                        g_, tt_ = its[i]
                        k_ = i % 2
                        ts_ = slice(tt_ * TT, (tt_ + 1) * TT)
                        kb.dma('sp', out=vgs[k_][:], in_=vg_tm[ts_, g_ * 128:(g_ + 1) * 128].rearrange("(n p) c -> p n c", p=128),
                               writes=[vgs[k_]], sem_of=vgs[k_])
                        kb.dma('sp', out=us[k_][:], in_=uT[g_ * 128:(g_ + 1) * 128, ts_], writes=[us[k_]], sem_of=us[k_])
                    ld(0)
                    for g in range(HS):
                        for tt in range(NT):
                            k = u % 2
                            u += 1
                            tsl = slice(tt * TT, (tt + 1) * TT)
                            if u < len(its):
                                ld(u)
                            pb = ps[k]
                            for n in range(4):
                                kb.op('pe', lambda pb=pb, n=n, k=k, g=g: nc.tensor.matmul(
                                    pb[:, n * 128:(n + 1) * 128], lhsT=vgs[k][:, n, :], rhs=wb[:, g, :], start=True, stop=True),
                                    reads=[vgs[k], wb], writes=[pb])
                            for n in range(4):
                                kb.op('dve', lambda pb=pb, n=n, k=k, g=g: nc.vector.tensor_tensor(
                                    out=tm[k][:, n * 128:(n + 1) * 128], in0=pb[:, n * 128:(n + 1) * 128], in1=bb[:, g, :], op=ALU.add),
                                    reads=[pb, bb], writes=[tm[k]])
                            kb.op('dve', lambda k=k: nc.vector.tensor_tensor(out=ob[k][:], in0=tm[k][:], in1=us[k][:], op=ALU.mult),
                                  reads=[tm[k], us[k]], writes=[ob[k]])
                            kb.dma('sp', out=hP[(HS + g) * 128:(HS + g + 1) * 128, tsl], in_=ob[k][:], reads=[ob[k]], sem_of=ob[k])

            def ffn_up(l):
                npass = max(1, FFC // 4)
                cpp = FFC // npass
                for p in range(npass):
                    with Phase(kb) as ph:
                        cs = [(p * cpp * 128, (p + 1) * cpp * 128), (FFPC + p * cpp * 128, FFPC + (p + 1) * cpp * 128)]
                        W = load_w(ph, I["fup"][l], KC, 2 * cpp * 128, "Wup", col_slices=cs)
                        fcw = ph.sbt([128, FFC * 3], F32, "fcw")
                        fcb = ph.sbt([128, FFC], F32, "fcb")
                        kb.dma('sp', out=fcw[:], in_=I["fcw"][l].rearrange("p a b -> p (a b)"), writes=[fcw], sem_of=fcw)
                        kb.dma('sp', out=fcb[:], in_=I["fcb"][l], writes=[fcb], sem_of=fcb)
                        hb = [ph.sbt([128, KC, TT], BF16, "fuh%d" % i) for i in range(2)]
                        gsb = [ph.sbt([128, 514], F32, "gsb%d" % i) for i in range(cpp)]
                        acc = [ph.sbt([128, 512], F32, "facc%d" % i) for i in range(2)]
                        ge = [ph.sbt([128, 512], F32, "fge%d" % i) for i in range(2)]
                        ao = [ph.sbt([128, 512], BF16, "fao%d" % i) for i in range(2)]
                        hv = hA.ap().rearrange("(kc p) t -> p kc t", p=128)
                        u = 0
                        def ld(tt):
                            kb.dma('sp', out=hb[tt % 2][:], in_=hv[:, :, tt * TT:(tt + 1) * TT], writes=[hb[tt % 2]], sem_of=hb[tt % 2])
                        ld(0)
                        for tt in range(NT):
                            k = tt % 2
                            tsl = slice(tt * TT, (tt + 1) * TT)
                            bstart = ((tt * TT) % S == 0)
                            if tt + 1 < NT:
                                ld(tt + 1)
                            for ci in range(cpp):
                                j = u % 2
                                u += 1
                                pG = ps[2 * j]
                                pV = ps[2 * j + 1]
                                for (pp, c0) in ((pG, ci * 128), (pV, (cpp + ci) * 128)):
                                    for kc in range(KC):
                                        kb.op('pe', lambda pp=pp, c0=c0, kc=kc, k=k: nc.tensor.matmul(
                                            pp[:, :], lhsT=W[kc // 4][:, kc % 4, c0:c0 + 128], rhs=hb[k][:, kc, :],
                                            start=(kc == 0), stop=(kc == KC - 1)), reads=[W[kc // 4], hb[k]], writes=[pp])
                                gs = gsb[ci]
                                if bstart:
                                    kb.op('pool', lambda gs=gs: nc.gpsimd.memset(gs[:, 0:2], 0.0), writes=[gs])
                                else:
                                    kb.op('pool', lambda gs=gs: nc.gpsimd.tensor_copy(out=gs[:, 0:2], in_=gs[:, 512:514]),
                                          reads=[gs], writes=[gs])
                                kb.op('act', lambda gs=gs, pG=pG: nc.scalar.activation(out=gs[:, 2:514], in_=pG[:, :], func=AF.Copy),
                                      reads=[pG], writes=[gs])
                                gch = p * cpp + ci
                                kb.op('dve', lambda gs=gs, j=j, gch=gch: nc.vector.tensor_scalar(
                                    out=acc[j][:], in0=gs[:, 2:514], scalar1=fcw[:, gch * 3 + 2:gch * 3 + 3], scalar2=fcb[:, gch:gch + 1],
                                    op0=ALU.mult, op1=ALU.add), reads=[gs, fcw, fcb], writes=[acc[j]])
                                for tap in (1, 0):
                                    kb.op('dve', lambda gs=gs, j=j, gch=gch, tap=tap: nc.vector.scalar_tensor_tensor(
                                        out=acc[j][:], in0=gs[:, tap:tap + 512], scalar=fcw[:, gch * 3 + tap:gch * 3 + tap + 1],
                                        op0=ALU.mult, in1=acc[j][:], op1=ALU.add), reads=[gs, fcw, acc[j]], writes=[acc[j]])
                                kb.op('act', lambda j=j: nc.scalar.activation(out=ge[j][:], in_=acc[j][:], func=AF.Gelu_apprx_tanh),
                                      reads=[acc[j]], writes=[ge[j]])
                                kb.op('dve', lambda j=j, pV=pV: nc.vector.tensor_tensor(out=ao[j][:], in0=ge[j][:], in1=pV[:, :], op=ALU.mult),
                                      reads=[ge[j], pV], writes=[ao[j]])
                                kb.dma('sp', out=aP[gch * 128:(gch + 1) * 128, tsl], in_=ao[j][:], reads=[ao[j]], sem_of=ao[j])
                kb.allgather(aP.ap(), aA.ap(), NC)

            qnT = dint("qnT", [gw, T], BF16, True); knT = dint("knT", [gw, T], BF16, True); szT = dint("szT", [gw, T], BF16, True)
            k_tm = dint("k_tm", [T, gw], BF16, True); v_tm2 = dint("v_tm2", [T, gw], BF16, True)
            pqc = {}

            def PQ(bank, q):
                if (bank, q) not in pqc:
                    t = Tl(None, "pq%d_%d" % (bank, q))
                    t.ap = ps[bank].t[:, q * 128:(q + 1) * 128]
                    t.R = ps[bank].R
                    pqc[(bank, q)] = t
                return pqc[(bank, q)]

            def conv_silu(ph_tiles, pb, cbuf, cw, widx, bstart, acc, sv, ntap=4):
                H_ = ntap - 1
                if bstart:
                    kb.op('pool', lambda: nc.gpsimd.memset(cbuf[:, 0:H_], 0.0), writes=[cbuf])
                else:
                    kb.op('pool', lambda: nc.gpsimd.tensor_copy(out=cbuf[:, 0:H_], in_=cbuf[:, 512:512 + H_]), reads=[cbuf], writes=[cbuf])
                kb.op('act', lambda: nc.scalar.activation(out=cbuf[:, H_:H_ + 512], in_=pb[:, :], func=AF.Copy), reads=[pb], writes=[cbuf])
                kb.op('dve', lambda: nc.vector.tensor_scalar(out=acc[:], in0=cbuf[:, H_:H_ + 512], scalar1=cw[:, widx + H_:widx + H_ + 1],
                                                             scalar2=None, op0=ALU.mult), reads=[cbuf, cw], writes=[acc])
                for j in range(H_ - 1, -1, -1):
                    kb.op('dve', lambda j=j: nc.vector.scalar_tensor_tensor(
                        out=acc[:], in0=cbuf[:, j:j + 512], scalar=cw[:, widx + j:widx + j + 1], op0=ALU.mult, in1=acc[:], op1=ALU.add),
                        reads=[cbuf, cw, acc], writes=[acc])
                kb.op('act', lambda: nc.scalar.activation(out=sv[:], in_=acc[:], func=AF.Silu), reads=[acc], writes=[sv])

            def gdn_mixer(o):
                hv = hA.ap().rearrange("(kc p) t -> p kc t", p=128)
                NBLK = T // 128
                with Phase(kb) as outer:
                    gb_all = outer.sbt([128, NBLK, 2 * HG], F32, "gball")
                    with Phase(kb) as ph:
                        W = load_w(ph, I["gin"][o], KC, 2 * gw, "Wg0", col_slices=[(0, 2 * gw)])
                        Wab = load_w(ph, I["gab"][o], KC, 2 * HG, "Wab")
                        cw = ph.sbt([128, 3 * HG * 4], F32, "gcw")
                        kb.dma('sp', out=cw[:], in_=I["gcw"][o].rearrange("p a b c -> p (a b c)"), writes=[cw], sem_of=cw)
                        gal = ph.sbt([128, HG], F32, "gal"); gdt = ph.sbt([128, HG], F32, "gdt"); negA = ph.sbt([128, HG], F32, "negA")
                        kb.dma('sp', out=gal[:], in_=I["gal"][o], writes=[gal], sem_of=gal)
                        kb.dma('sp', out=gdt[:], in_=I["gdt"][o], writes=[gdt], sem_of=gdt)
                        kb.op('act', lambda: nc.scalar.activation(out=negA[:], in_=gal[:], func=AF.Exp), reads=[gal], writes=[negA])
                        kb.op('dve', lambda: nc.vector.tensor_scalar(out=negA[:], in0=negA[:], scalar1=-1.0, scalar2=None, op0=ALU.mult),
                              reads=[negA], writes=[negA])
                        onesb = ph.sbt([128, 128], BF16, "g_onesb")
                        kb.op('pool', lambda: nc.gpsimd.memset(onesb[:], 1.0), writes=[onesb])
                        hb = [ph.sbt([128, KC, TT], BF16, "gh%d" % i) for i in range(2)]
                        cb = [ph.sbt([128, 515], F32, "gcb%d" % i) for i in range(2 * HG)]
                        acc = [ph.sbt([128, 512], F32, "gacc%d" % i) for i in range(2)]
                        sv = [ph.sbt([128, 512], F32, "gsv%d" % i) for i in range(2)]
                        sq = [ph.sbt([128, 512], BF16, "gsq%d" % i) for i in range(2)]
                        rs = [ph.sbt([128, 512], F32, "grs%d" % i) for i in range(2)]
                        xnf = [ph.sbt([128, 512], F32, "gxnf%d" % i) for i in range(2)]
                        xnb = [ph.sbt([128, 512], BF16, "gxnb%d" % i) for i in range(2)]
                        ktm = [ph.sbt([128, 4, 128], BF16, "gktm%d" % i) for i in range(2)]
                        sm = [[ph.sbt([128, HG], F32, "gsm%d_%d" % (a, i)) for i in range(2)] for a in range(4)]
                        u = 0
                        def ld(tt):
                            kb.dma('sp', out=hb[tt % 2][:], in_=hv[:, :, tt * TT:(tt + 1) * TT], writes=[hb[tt % 2]], sem_of=hb[tt % 2])
                        ld(0)
                        for tt in range(NT):
                            k = tt % 2
                            tsl = slice(tt * TT, (tt + 1) * TT)
                            bstart = ((tt * TT) % S == 0)
                            if tt + 1 < NT:
                                ld(tt + 1)
                            for ci in range(2 * HG):
                                s_ = ci // HG
                                h = ci % HG
                                j = u % 2
                                u += 1
                                pb = ps[j]
                                for kc in range(KC):
                                    kb.op('pe', lambda pb=pb, kc=kc, ci=ci, k=k: nc.tensor.matmul(
                                        pb[:, :], lhsT=W[kc // 4][:, kc % 4, ci * 128:(ci + 1) * 128], rhs=hb[k][:, kc, :],
                                        start=(kc == 0), stop=(kc == KC - 1)), reads=[W[kc // 4], hb[k]], writes=[pb])
                                conv_silu(None, pb, cb[ci], cw, (s_ * HG + h) * 4, bstart, acc[j], sv[j])
                                kb.op('act', lambda j=j: nc.scalar.activation(out=sq[j][:], in_=sv[j][:], func=AF.Square), reads=[sv[j]], writes=[sq[j]])
                                pn = ps[2 + j]
                                kb.op('pe', lambda pn=pn, j=j: nc.tensor.matmul(pn[:, :], lhsT=onesb[:], rhs=sq[j][:], start=True, stop=True),
                                      reads=[onesb, sq[j]], writes=[pn])
                                kb.op('act', lambda pn=pn, j=j: nc.scalar.activation(out=rs[j][:], in_=pn[:, :], func=AF.Sqrt, bias=epsc[:, 0:1]),
                                      reads=[pn, epsc], writes=[rs[j]])
                                kb.op('dve', lambda j=j: nc.vector.reciprocal(out=rs[j][:], in_=rs[j][:]), reads=[rs[j]], writes=[rs[j]])
                                scl = (128.0 ** -0.5) if s_ == 0 else 1.0
                                kb.op('dve', lambda j=j, scl=scl: nc.vector.scalar_tensor_tensor(
                                    out=xnf[j][:], in0=sv[j][:], scalar=scl, op0=ALU.mult, in1=rs[j][:], op1=ALU.mult),
                                    reads=[sv[j], rs[j]], writes=[xnf[j]])
                                kb.op('act', lambda j=j: nc.scalar.activation(out=xnb[j][:], in_=xnf[j][:], func=AF.Copy), reads=[xnf[j]], writes=[xnb[j]])
                                dst = qnT if s_ == 0 else knT
                                kb.dma('sp', out=dst[h * 128:(h + 1) * 128, tsl], in_=xnb[j][:], reads=[xnb[j]], sem_of=xnb[j])
                                if s_ == 1:
                                    pt = ps[4 + j]
                                    for n in range(4):
                                        kb.op('pe', lambda pt=pt, n=n, j=j: nc.tensor.transpose(
                                            out=pt[:, n * 128:(n + 1) * 128], in_=xnf[j][:, n * 128:(n + 1) * 128], identity=ident[:]),
                                            reads=[xnf[j], ident], writes=[pt])
                                    kb.op('act', lambda pt=pt, j=j: nc.scalar.activation(
                                        out=ktm[j][:].rearrange("p a b -> p (a b)"), in_=pt[:, :], func=AF.Copy), reads=[pt], writes=[ktm[j]])
                                    kb.dma('sp', out=k_tm[tsl, h * 128:(h + 1) * 128].rearrange("(n p) d -> p n d", p=128), in_=ktm[j][:],
                                           reads=[ktm[j]], sem_of=ktm[j])
                            for sub in range(4):
                                j = sub % 2
                                pab = ps[6 + j]
                                for kc in range(KC):
                                    kb.op('pe', lambda pab=pab, kc=kc, sub=sub, k=k: nc.tensor.matmul(
                                        pab[:, 0:2 * HG], lhsT=hb[k][:, kc, sub * 128:(sub + 1) * 128], rhs=Wab[kc // 4][:, kc % 4, :],
                                        start=(kc == 0), stop=(kc == KC - 1)), reads=[Wab[kc // 4], hb[k]], writes=[pab])
                                blk = tt * 4 + sub
                                t_, e_, sp_, e2 = sm[0][j], sm[1][j], sm[2][j], sm[3][j]
                                kb.op('dve', lambda pab=pab, t_=t_: nc.vector.tensor_tensor(out=t_[:], in0=pab[:, 0:HG], in1=gdt[:], op=ALU.add),
                                      reads=[pab, gdt], writes=[t_])
                                kb.op('act', lambda t_=t_, e_=e_: nc.scalar.activation(out=e_[:], in_=t_[:], func=AF.Exp), reads=[t_], writes=[e_])
                                kb.op('act', lambda sp_=sp_, e_=e_: nc.scalar.activation(out=sp_[:], in_=e_[:], func=AF.Ln, bias=1.0), reads=[e_], writes=[sp_])
                                kb.op('dve', lambda sp_=sp_, blk=blk: nc.vector.tensor_tensor(out=gb_all[:, blk, 0:HG], in0=sp_[:], in1=negA[:], op=ALU.mult),
                                      reads=[sp_, negA], writes=[gb_all])
                                kb.op('act', lambda pab=pab, e2=e2: nc.scalar.activation(out=e2[:], in_=pab[:, HG:2 * HG], func=AF.Exp, scale=-1.0),
                                      reads=[pab], writes=[e2])
                                kb.op('dve', lambda e2=e2: nc.vector.tensor_scalar(out=e2[:], in0=e2[:], scalar1=1.0, scalar2=None, op0=ALU.add),
                                      reads=[e2], writes=[e2])
                                kb.op('dve', lambda e2=e2, blk=blk: nc.vector.reciprocal(out=gb_all[:, blk, HG:2 * HG], in_=e2[:]),
                                      reads=[e2], writes=[gb_all])
                    with Phase(kb) as ph:
                        W = load_w(ph, I["gin"][o], KC, 2 * gw, "Wg1", col_slices=[(2 * gw, 4 * gw)])
                        cw = ph.sbt([128, 3 * HG * 4], F32, "gcw1")
                        kb.dma('sp', out=cw[:], in_=I["gcw"][o].rearrange("p a b c -> p (a b c)"), writes=[cw], sem_of=cw)
                        hb = [ph.sbt([128, KC, TT], BF16, "g1h%d" % i) for i in range(2)]
                        cb = [ph.sbt([128, 515], F32, "g1cb%d" % i) for i in range(HG)]
                        acc = [ph.sbt([128, 512], F32, "g1acc%d" % i) for i in range(2)]
                        sv = [ph.sbt([128, 512], F32, "g1sv%d" % i) for i in range(2)]
                        vtm = [ph.sbt([128, 4, 128], BF16, "g1vtm%d" % i) for i in range(2)]
                        zb = [ph.sbt([128, 512], BF16, "g1zb%d" % i) for i in range(2)]
                        u = 0
                        def ld(tt):
                            kb.dma('sp', out=hb[tt % 2][:], in_=hv[:, :, tt * TT:(tt + 1) * TT], writes=[hb[tt % 2]], sem_of=hb[tt % 2])
                        ld(0)
                        for tt in range(NT):
                            k = tt % 2
                            tsl = slice(tt * TT, (tt + 1) * TT)
                            bstart = ((tt * TT) % S == 0)
                            if tt + 1 < NT:
                                ld(tt + 1)
                            for ci in range(2 * HG):
                                h = ci % HG
                                j = u % 2
                                u += 1
                                pb = ps[j]
                                for kc in range(KC):
                                    kb.op('pe', lambda pb=pb, kc=kc, ci=ci, k=k: nc.tensor.matmul(
                                        pb[:, :], lhsT=W[kc // 4][:, kc % 4, ci * 128:(ci + 1) * 128], rhs=hb[k][:, kc, :],
                                        start=(kc == 0), stop=(kc == KC - 1)), reads=[W[kc // 4], hb[k]], writes=[pb])
                                if ci < HG:
                                    conv_silu(None, pb, cb[h], cw, (2 * HG + h) * 4, bstart, acc[j], sv[j])
                                    pt = ps[4 + j]
                                    for n in range(4):
                                        kb.op('pe', lambda pt=pt, n=n, j=j: nc.tensor.transpose(
                                            out=pt[:, n * 128:(n + 1) * 128], in_=sv[j][:, n * 128:(n + 1) * 128], identity=ident[:]),
                                            reads=[sv[j], ident], writes=[pt])
                                    kb.op('act', lambda pt=pt, j=j: nc.scalar.activation(
                                        out=vtm[j][:].rearrange("p a b -> p (a b)"), in_=pt[:, :], func=AF.Copy), reads=[pt], writes=[vtm[j]])
                                    kb.dma('sp', out=v_tm2[tsl, h * 128:(h + 1) * 128].rearrange("(n p) d -> p n d", p=128), in_=vtm[j][:],
                                           reads=[vtm[j]], sem_of=vtm[j])
                                else:
                                    kb.op('act', lambda pb=pb, j=j: nc.scalar.activation(out=zb[j][:], in_=pb[:, :], func=AF.Silu), reads=[pb], writes=[zb[j]])
                                    kb.dma('sp', out=szT[h * 128:(h + 1) * 128, tsl], in_=zb[j][:], reads=[zb[j]], sem_of=zb[j])
                    with Phase(kb) as ph:
                        def ctile(name, src):
                            t = ph.sbt(list(src.shape), F32, name)
                            kb.dma('sp', out=t[:], in_=src, writes=[t], sem_of=t)
                            return t
                        ltri = ctile("c_ltri", I["ltri_le"].ap())
                        negm = ctile("c_negm", I["negmaskT"].ap())
                        strT = ctile("c_strT", I["strictT"].ap())
                        lvm = ctile("c_lvm", I["lvmask"].ap())
                        gog = ctile("c_gog", I["gog"][o])
                        ones = ph.sbt([128, 128], F32, "c_ones")
                        kb.op('pool', lambda: nc.gpsimd.memset(ones[:], 1.0), writes=[ones])
                        Sf = [ph.sbt([128, 128], F32, "Sf%d" % h) for h in range(HG)]
                        Sb = [ph.sbt([128, 128], BF16, "Sb%d" % h) for h in range(HG)]
                        obuf = [ph.sbt([128, 512], BF16, "gob%d" % h) for h in range(HG)]
                        qc = [[ph.sbt([128, 512], BF16, "gqc%d_%d" % (h, i)) for i in range(2)] for h in range(HG)]
                        kc_ = [[ph.sbt([128, 512], BF16, "gkc%d_%d" % (h, i)) for i in range(2)] for h in range(HG)]
                        szc = [[ph.sbt([128, 512], BF16, "gsz%d_%d" % (h, i)) for i in range(2)] for h in range(HG)]
                        ktc = [[ph.sbt([128, 4, 128], BF16, "gkt%d_%d" % (h, i)) for i in range(2)] for h in range(HG)]
                        vtc = [[ph.sbt([128, 4, 128], BF16, "gvt%d_%d" % (h, i)) for i in range(2)] for h in range(HG)]
                        gsm = [ph.sbt([128, 2 * HG], F32, "ggsm%d" % i) for i in range(2)]
                        smx = [[ph.sbt([128, HG], F32, "gsx%d_%d" % (a, i)) for i in range(2)] for a in range(5)]

                        def wt2(name, dt=F32):
                            return [ph.sbt([128, 128], dt, "%s%d" % (name, i)) for i in range(2)]
                        Gb, egcb, arg, DT, DTs, ATr, Am = (wt2(n) for n in ("wGb", "wegcb", "warg", "wDT", "wDTs", "wATr", "wA"))
                        qkTm = wt2("wqkTm", BF16)
                        Al = [wt2("wAl%d" % l) for l in range(7)]
                        Psb, Y, Z, vb_, kbe, usb, on_, junk = (wt2(n) for n in ("wPsb", "wY", "wZ", "wvb", "wkbe", "wusb", "won", "wjunk"))
                        wTb, kdec, qdT, vnew = (wt2(n, BF16) for n in ("wwTb", "wkdec", "wqdT", "wvnew"))
                        ssq = [ph.sbt([128, 1], F32, "gssq%d" % i) for i in range(2)]
                        un = 0
                        gi = 0
                        NG = S // 512

                        def ldg(gix):
                            lk_ = gix % 2
                            t0_ = (gix // NG) * S + (gix % NG) * 512
                            ts_ = slice(t0_, t0_ + 512)
                            for h in range(HG):
                                hs = slice(h * 128, (h + 1) * 128)
                                kb.dma('sp', out=qc[h][lk_][:], in_=qnT[hs, ts_], writes=[qc[h][lk_]], sem_of=qc[h][lk_])
                                kb.dma('sp', out=kc_[h][lk_][:], in_=knT[hs, ts_], writes=[kc_[h][lk_]], sem_of=kc_[h][lk_])
                                kb.dma('sp', out=szc[h][lk_][:], in_=szT[hs, ts_], writes=[szc[h][lk_]], sem_of=szc[h][lk_])
                                kb.dma('sp', out=ktc[h][lk_][:], in_=k_tm[ts_, hs].rearrange("(n p) d -> p n d", p=128), writes=[ktc[h][lk_]], sem_of=ktc[h][lk_])
                                kb.dma('sp', out=vtc[h][lk_][:], in_=v_tm2[ts_, hs].rearrange("(n p) d -> p n d", p=128), writes=[vtc[h][lk_]], sem_of=vtc[h][lk_])
                        ldg(0)
                        for b in range(B):
                            for h in range(HG):
                                kb.op('pool', lambda h=h: nc.gpsimd.memset(Sf[h][:], 0.0), writes=[Sf[h]])
                                kb.op('pool', lambda h=h: nc.gpsimd.memset(Sb[h][:], 0.0), writes=[Sb[h]])
                            for tg in range(S // 512):
                                lk = gi % 2
                                gi += 1
                                t0 = b * S + tg * 512
                                tsl = slice(t0, t0 + 512)
                                if gi < B * NG:
                                    ldg(gi)
                                for n4 in range(4):
                                    blk = t0 // 128 + n4
                                    g2 = blk % 2
                                    csl = slice(n4 * 128, (n4 + 1) * 128)
                                    gq = PQ(0, g2)
                                    gsm_ = gsm[g2]
                                    kb.op('pe', lambda gq=gq, blk=blk: nc.tensor.matmul(gq.ap[:, 0:HG], lhsT=ltri[:], rhs=gb_all[:, blk, 0:HG], start=True, stop=True),
                                          reads=[ltri, gb_all], writes=[gq])
                                    kb.op('pe', lambda gq=gq, blk=blk: nc.tensor.matmul(gq.ap[:, HG:2 * HG], lhsT=ones[:], rhs=gb_all[:, blk, 0:HG], start=True, stop=True),
                                          reads=[ones, gb_all], writes=[gq])
                                    kb.op('dve', lambda gq=gq, gsm_=gsm_: nc.vector.tensor_copy(out=gsm_[:], in_=gq.ap[:, 0:2 * HG]), reads=[gq], writes=[gsm_])
                                    egc, be, dd, kdsc, gtot = (smx[a][g2] for a in range(5))
                                    kb.op('act', lambda egc=egc, gsm_=gsm_: nc.scalar.activation(out=egc[:], in_=gsm_[:, 0:HG], func=AF.Exp), reads=[gsm_], writes=[egc])
                                    kb.op('dve', lambda be=be, egc=egc, blk=blk: nc.vector.tensor_tensor(out=be[:], in0=egc[:], in1=gb_all[:, blk, HG:2 * HG], op=ALU.mult),
                                          reads=[egc, gb_all], writes=[be])
                                    kb.op('dve', lambda dd=dd, gsm_=gsm_: nc.vector.tensor_tensor(out=dd[:], in0=gsm_[:, HG:2 * HG], in1=gsm_[:, 0:HG], op=ALU.subtract),
                                          reads=[gsm_], writes=[dd])
                                    kb.op('act', lambda dd=dd, kdsc=kdsc: nc.scalar.activation(out=kdsc[:], in_=dd[:], func=AF.Exp), reads=[dd], writes=[kdsc])
                                    kb.op('act', lambda gtot=gtot, gsm_=gsm_: nc.scalar.activation(out=gtot[:], in_=gsm_[:, HG:2 * HG], func=AF.Exp), reads=[gsm_], writes=[gtot])
                                    for h in range(HG):
                                        w = un % 2
                                        un += 1
                                        B1 = 1 + 3 * w
                                        q_c = qc[h][lk]; k_c = kc_[h][lk]; kt = ktc[h][lk]; vt = vtc[h][lk]; sz = szc[h][lk]
                                        kb.op('pool', lambda w=w, blk=blk, h=h: nc.gpsimd.tensor_scalar(
                                            out=Gb[w][:], in0=ones[:], scalar1=gb_all[:, blk, h:h + 1], scalar2=None, op0=ALU.mult),
                                            reads=[ones, gb_all], writes=[Gb[w]])
                                        p_gc = PQ(B1, 0)
                                        kb.op('pe', lambda p_gc=p_gc, w=w: nc.tensor.matmul(p_gc.ap, lhsT=Gb[w][:], rhs=ltri[:], start=True, stop=True),
                                              reads=[Gb[w], ltri], writes=[p_gc])
                                        kb.op('act', lambda p_gc=p_gc, w=w: nc.scalar.activation(out=egcb[w][:], in_=p_gc.ap, func=AF.Exp), reads=[p_gc], writes=[egcb[w]])
                                        kb.op('dve', lambda p_gc=p_gc, w=w, gsm_=gsm_, h=h: nc.vector.scalar_tensor_tensor(
                                            out=arg[w][:], in0=negm[:], scalar=gsm_[:, h:h + 1], op0=ALU.subtract, in1=p_gc.ap, op1=ALU.add),
                                            reads=[negm, gsm_, p_gc], writes=[arg[w]])
                                        kb.op('act', lambda w=w: nc.scalar.activation(out=DT[w][:], in_=arg[w][:], func=AF.Exp), reads=[arg[w]], writes=[DT[w]])
                                        kb.op('pool', lambda w=w: nc.gpsimd.tensor_tensor(out=DTs[w][:], in0=DT[w][:], in1=strT[:], op=ALU.mult),
                                              reads=[DT[w], strT], writes=[DTs[w]])
                                        p_G = PQ(B1, 1)
                                        kb.op('pe', lambda p_G=p_G, k_c=k_c, csl=csl: nc.tensor.matmul(p_G.ap, lhsT=k_c[:, csl], rhs=k_c[:, csl], start=True, stop=True),
                                              reads=[k_c], writes=[p_G])
                                        kb.op('dve', lambda p_G=p_G, w=w: nc.vector.tensor_tensor(out=ATr[w][:], in0=p_G.ap, in1=DTs[w][:], op=ALU.mult),
                                              reads=[p_G, DTs[w]], writes=[ATr[w]])
                                        p_qk = PQ(B1, 2)
                                        kb.op('pe', lambda p_qk=p_qk, k_c=k_c, q_c=q_c, csl=csl: nc.tensor.matmul(p_qk.ap, lhsT=k_c[:, csl], rhs=q_c[:, csl], start=True, stop=True),
                                              reads=[k_c, q_c], writes=[p_qk])
                                        kb.op('dve', lambda p_qk=p_qk, w=w: nc.vector.tensor_tensor(out=qkTm[w][:], in0=p_qk.ap, in1=DT[w][:], op=ALU.mult),
                                              reads=[p_qk, DT[w]], writes=[qkTm[w]])
                                        p_A = PQ(B1, 3)
                                        kb.op('pe', lambda p_A=p_A, w=w: nc.tensor.transpose(out=p_A.ap, in_=ATr[w][:], identity=ident[:]),
                                              reads=[ATr[w], ident], writes=[p_A])
                                        kb.op('dve', lambda p_A=p_A, w=w, blk=blk, h=h: nc.vector.tensor_scalar(
                                            out=Am[w][:], in0=p_A.ap, scalar1=gb_all[:, blk, HG + h:HG + h + 1], scalar2=None, op0=ALU.mult),
                                            reads=[p_A, gb_all], writes=[Am[w]])
                                        for l in range(7):
                                            kb.op('pool', lambda l=l, w=w: nc.gpsimd.tensor_tensor(out=Al[l][w][:], in0=Am[w][:], in1=lvm[:, l, :], op=ALU.mult),
                                                  reads=[Am[w], lvm], writes=[Al[l][w]])
                                        p_t = PQ(B1 + 1, 3)
                                        kb.op('pe', lambda p_t=p_t, w=w: nc.tensor.transpose(out=p_t.ap, in_=Al[0][w][:], identity=ident[:]),
                                              reads=[Al[0][w], ident], writes=[p_t])
                                        kb.op('dve', lambda p_t=p_t, w=w: nc.vector.tensor_tensor(out=Y[w][:], in0=ident[:], in1=p_t.ap, op=ALU.subtract),
                                              reads=[ident, p_t], writes=[Y[w]])
                                        kb.op('pool', lambda w=w: nc.gpsimd.tensor_tensor(out=Z[w][:], in0=ident[:], in1=Al[0][w][:], op=ALU.subtract),
                                              reads=[ident, Al[0][w]], writes=[Z[w]])
                                        for l in range(1, 7):
                                            p_P = PQ(B1 + 1, 0); p_Q = PQ(B1 + 1, 1); p_Qn = PQ(B1 + 1, 2)
                                            kb.op('pe', lambda p_P=p_P, l=l, w=w: nc.tensor.matmul(p_P.ap, lhsT=Al[l][w][:], rhs=Y[w][:], start=True, stop=True),
                                                  reads=[Al[l][w], Y[w]], writes=[p_P])
                                            kb.op('act', lambda p_P=p_P, w=w: nc.scalar.activation(out=Psb[w][:], in_=p_P.ap, func=AF.Copy), reads=[p_P], writes=[Psb[w]])
                                            kb.op('pe', lambda p_Q=p_Q, w=w: nc.tensor.matmul(p_Q.ap, lhsT=Z[w][:], rhs=Psb[w][:], start=True, stop=True),
                                                  reads=[Z[w], Psb[w]], writes=[p_Q])
                                            if l < 6:
                                                kb.op('pe', lambda p_Qn=p_Qn, w=w: nc.tensor.matmul(p_Qn.ap, lhsT=Psb[w][:], rhs=Z[w][:], start=True, stop=True),
                                                      reads=[Z[w], Psb[w]], writes=[p_Qn])
                                            kb.op('dve', lambda p_Q=p_Q, w=w: nc.vector.tensor_tensor(out=Y[w][:], in0=Y[w][:], in1=p_Q.ap, op=ALU.subtract),
                                                  reads=[Y[w], p_Q], writes=[Y[w]])
                                            if l < 6:
                                                kb.op('dve', lambda p_Qn=p_Qn, w=w: nc.vector.tensor_tensor(out=Z[w][:], in0=Z[w][:], in1=p_Qn.ap, op=ALU.subtract),
                                                      reads=[Z[w], p_Qn], writes=[Z[w]])
                                        kb.op('act', lambda w=w, vt=vt, n4=n4, blk=blk, h=h: nc.scalar.activation(
                                            out=vb_[w][:], in_=vt[:, n4, :], func=AF.Copy, scale=gb_all[:, blk, HG + h:HG + h + 1]),
                                            reads=[vt, gb_all], writes=[vb_[w]])
                                        kb.op('act', lambda w=w, kt=kt, n4=n4, be=be, h=h: nc.scalar.activation(
                                            out=kbe[w][:], in_=kt[:, n4, :], func=AF.Copy, scale=be[:, h:h + 1]), reads=[kt, be], writes=[kbe[w]])
                                        p_u = PQ(B1 + 2, 0); p_w = PQ(B1 + 2, 1)
                                        kb.op('pe', lambda p_u=p_u, w=w: nc.tensor.matmul(p_u.ap, lhsT=Y[w][:], rhs=vb_[w][:], start=True, stop=True),
                                              reads=[Y[w], vb_[w]], writes=[p_u])
                                        kb.op('pe', lambda p_w=p_w, w=w: nc.tensor.matmul(p_w.ap, lhsT=kbe[w][:], rhs=Y[w][:], start=True, stop=True),
                                              reads=[Y[w], kbe[w]], writes=[p_w])
                                        kb.op('act', lambda p_u=p_u, w=w: nc.scalar.activation(out=usb[w][:], in_=p_u.ap, func=AF.Copy), reads=[p_u], writes=[usb[w]])
                                        kb.op('act', lambda p_w=p_w, w=w: nc.scalar.activation(out=wTb[w][:], in_=p_w.ap, func=AF.Copy), reads=[p_w], writes=[wTb[w]])
                                        kb.op('pool', lambda w=w, kt=kt, n4=n4, kdsc=kdsc, h=h: nc.gpsimd.tensor_scalar(
                                            out=kdec[w][:], in0=kt[:, n4, :], scalar1=kdsc[:, h:h + 1], scalar2=None, op0=ALU.mult),
                                            reads=[kt, kdsc], writes=[kdec[w]])
                                        kb.op('dve', lambda w=w, q_c=q_c, csl=csl: nc.vector.tensor_tensor(out=qdT[w][:], in0=q_c[:, csl], in1=egcb[w][:], op=ALU.mult),
                                              reads=[q_c, egcb[w]], writes=[qdT[w]])
                                        p_1 = PQ(B1 + 2, 2); p_o = PQ(B1 + 2, 3); p_3 = PQ(7, w); p_T = PQ(7, 2 + w)
                                        kb.op('pe', lambda p_1=p_1, w=w, h=h: nc.tensor.matmul(p_1.ap, lhsT=wTb[w][:], rhs=Sb[h][:], start=True, stop=True),
                                              reads=[wTb[w], Sb[h]], writes=[p_1])
                                        kb.op('dve', lambda p_1=p_1, w=w: nc.vector.tensor_tensor(out=vnew[w][:], in0=usb[w][:], in1=p_1.ap, op=ALU.subtract),
                                              reads=[usb[w], p_1], writes=[vnew[w]])
                                        kb.op('pe', lambda p_o=p_o, w=w, h=h: nc.tensor.matmul(p_o.ap, lhsT=qdT[w][:], rhs=Sb[h][:], start=True, stop=False),
                                              reads=[qdT[w], Sb[h]], writes=[p_o])
                                        kb.op('pe', lambda p_o=p_o, w=w: nc.tensor.matmul(p_o.ap, lhsT=qkTm[w][:], rhs=vnew[w][:], start=False, stop=True),
                                              reads=[qkTm[w], vnew[w]], writes=[p_o])
                                        kb.op('pe', lambda p_3=p_3, w=w: nc.tensor.matmul(p_3.ap, lhsT=kdec[w][:], rhs=vnew[w][:], start=True, stop=True),
                                              reads=[kdec[w], vnew[w]], writes=[p_3])
                                        kb.op('dve', lambda p_3=p_3, h=h, gtot=gtot: nc.vector.scalar_tensor_tensor(
                                            out=Sf[h][:], in0=Sf[h][:], scalar=gtot[:, h:h + 1], op0=ALU.mult, in1=p_3.ap, op1=ALU.add),
                                            reads=[Sf[h], gtot, p_3], writes=[Sf[h]])
                                        kb.op('act', lambda h=h: nc.scalar.activation(out=Sb[h][:], in_=Sf[h][:], func=AF.Copy), reads=[Sf[h]], writes=[Sb[h]])
                                        kb.op('act', lambda p_o=p_o, w=w: nc.scalar.activation(out=junk[w][:], in_=p_o.ap, func=AF.Square, accum_out=ssq[w][:, 0:1]),
                                              reads=[p_o], writes=[junk[w], ssq[w]])
                                        kb.op('act', lambda w=w: nc.scalar.activation(out=ssq[w][:], in_=ssq[w][:], func=AF.Sqrt, scale=1.0 / 128, bias=epsc[:, 0:1]),
                                              reads=[ssq[w], epsc], writes=[ssq[w]])
                                        kb.op('dve', lambda w=w: nc.vector.reciprocal(out=ssq[w][:], in_=ssq[w][:]), reads=[ssq[w]], writes=[ssq[w]])
                                        kb.op('dve', lambda p_o=p_o, w=w: nc.vector.tensor_scalar(out=on_[w][:], in0=p_o.ap, scalar1=ssq[w][:, 0:1], scalar2=None, op0=ALU.mult),
                                              reads=[p_o, ssq[w]], writes=[on_[w]])
                                        kb.op('pe', lambda p_T=p_T, w=w: nc.tensor.transpose(out=p_T.ap, in_=on_[w][:], identity=ident[:]),
                                              reads=[on_[w], ident], writes=[p_T])
                                        kb.op('dve', lambda p_T=p_T, h=h, sz=sz, csl=csl: nc.vector.scalar_tensor_tensor(
                                            out=obuf[h][:, csl], in0=p_T.ap, scalar=gog[:, 0:1], op0=ALU.mult, in1=sz[:, csl], op1=ALU.mult),
                                            reads=[p_T, gog, sz], writes=[obuf[h]])
                                for h in range(HG):
                                    kb.dma('sp', out=hP[h * 128:(h + 1) * 128, tsl], in_=obuf[h][:], reads=[obuf[h]], sem_of=obuf[h])


            for l in range(DEPTH):
                ss_phase()
                norm_phase(lambda b, fc, l=l: AM.t[:, (l * B + b) * FC + fc:(l * B + b) * FC + fc + 1],
                           lambda b, fc, l=l: mcol(l, b, 0, fc), hP, BF16, dbg_h[2 * l] if dbg else None)
                kb.allgather(hP.ap(), hA.ap(), NC)
                if l % 2 == 0:
                    e = l // 2
                    even_inproj(e)
                    sb_attention()
                    sgu(e)
                    if dbg:
                        kb.dma('sp', out=dbg_o[l], in_=hP[:, :], sem_of=dscr)
                    kb.allgather(hP.ap(), hA.ap(), NC)
                    proj_res_phase(I["evout"][e], hA, KC, l, 2, 512, dbg_x[2 * l] if dbg else None)
                else:
                    o = l // 2
                    gdn_mixer(o)
                    if dbg:
                        kb.dma('sp', out=dbg_o[l], in_=hP[:, :], sem_of=dscr)
                    kb.allgather(hP.ap(), hA.ap(), NC)
                    proj_res_phase(I["gout"][o], hA, KC, l, 2, 512, dbg_x[2 * l] if dbg else None)
                ss_phase()
                norm_phase(lambda b, fc, l=l: AFN.t[:, (l * B + b) * FC + fc:(l * B + b) * FC + fc + 1],
                           lambda b, fc, l=l: mcol(l, b, 3, fc), hP, BF16, dbg_h[2 * l + 1] if dbg else None)
                kb.allgather(hP.ap(), hA.ap(), NC)
                ffn_up(l)
                proj_res_phase(I["fdn"][l], aA, KC2, l, 5, 256, dbg_x[2 * l + 1] if dbg else None)
            ss_phase()
            norm_phase(lambda b, fc: nfin.t[:, fc:fc + 1], lambda b, fc: None, yT, F32)
            kb.barrier()
    print("instructions emitted:", kb.ninst)
    return nc


def run(cfg, inputs, dbg=False, trace=False):
    maps = prep_inputs(cfg, inputs)
    nc = build(cfg, dbg)
    names = set()
    for alloc in nc.allocations:
        pass
    res = run_bass_kernel_spmd(nc, maps, core_ids=list(range(cfg.NC)), trace=trace)
    outs = res.results
    y = np.concatenate([np.asarray(r["yT"]) for r in outs], 0)
    y = np.ascontiguousarray(y.T).reshape(cfg.B, cfg.S, cfg.D)
    return y, outs, res


def kernel(**inputs):
    cfg = Cfg()
    y, _, _ = run(cfg, inputs)
    return np.ascontiguousarray(y, dtype=np.float32)
```
